# Optimizing a Trainium2 kernel written in Bass

```python
import jax, jax.numpy as jnp
from jax import lax
import numpy as np

D_MODEL = 2048
BATCH = 4
SEQ = 4096
DEPTH = 2
DEC_BATCH = 8
DEC_SEQ = 4096
PAST_LEN = 128

D_FF = 5632
N_AB_LAYERS = (DEPTH + 1) // 2
N_C_LAYERS = DEPTH // 2
MLA_HEADS = 8
Q_LORA = 512
KV_LORA = 512
QK_NOPE = 128
QK_ROPE = 64
QK_HEAD = QK_NOPE + QK_ROPE
V_HEAD = 128
MLA_WIDTH = MLA_HEADS * V_HEAD
CONV_WIDTH = D_MODEL - MLA_WIDTH
CONV_K = 3
AB_IN = Q_LORA + KV_LORA + QK_ROPE + 3 * CONV_WIDTH
AB_SPLITS = [Q_LORA, Q_LORA + KV_LORA, Q_LORA + KV_LORA + QK_ROPE,
             Q_LORA + KV_LORA + QK_ROPE + CONV_WIDTH,
             Q_LORA + KV_LORA + QK_ROPE + 2 * CONV_WIDTH]
C_HEADS = 16
C_KV_HEADS = 4
C_GROUP = C_HEADS // C_KV_HEADS
C_HEAD_DIM = 128
C_IN = (C_HEADS + 2 * C_KV_HEADS) * C_HEAD_DIM
WINDOW = 128
BLOCK = 128
ROPE_THETA = 10000.0
EPS = 1e-6
NEG = -1e30

kernel_name = "hybrid_mla_shortconv_swa_macaron_encoder"


def rms_norm(x, g):
    xf = x.astype(jnp.float32)
    y = xf * lax.rsqrt(jnp.mean(xf * xf, axis=-1, keepdims=True) + EPS)
    return (y * g.astype(jnp.float32)).astype(x.dtype)


def rope_tables(seq_len, dim):
    inv = 1.0 / (ROPE_THETA ** (jnp.arange(0, dim, 2, dtype=jnp.float32) / dim))
    ang = jnp.arange(seq_len, dtype=jnp.float32)[:, None] * inv[None, :]
    return jnp.cos(ang), jnp.sin(ang)


def apply_rope(x, cos, sin):
    half = x.shape[-1] // 2
    xf = x.astype(jnp.float32)
    x1, x2 = xf[..., :half], xf[..., half:]
    c, s = cos[:, None, :], sin[:, None, :]
    return jnp.concatenate([x1 * c - x2 * s, x2 * c + x1 * s], axis=-1).astype(x.dtype)


def swiglu(x, w_gate, w_up, w_down):
    return (jax.nn.silu(x @ w_gate) * (x @ w_up)) @ w_down


def dense_attention(q, k, v, scale):
    bsz, s_len, h, dq = q.shape
    nb = s_len // BLOCK
    q_blocks = q.reshape(bsz, nb, BLOCK, h, dq).transpose(1, 0, 2, 3, 4)

    def one_block(qb):
        s = jnp.einsum('bqhd,bkhd->bhqk', qb, k).astype(jnp.float32) * scale
        p = jax.nn.softmax(s, axis=-1).astype(v.dtype)
        return jnp.einsum('bhqk,bkhd->bqhd', p, v)

    out = lax.map(one_block, q_blocks)
    return out.transpose(1, 0, 2, 3, 4).reshape(bsz, s_len, h, v.shape[-1])


def mla_conv_mixer(h, w_in, q_a_norm, w_q_b, kv_a_norm, w_kv_b, q_norm, k_norm, conv_w, w_out):
    bsz, s_len, _ = h.shape
    z = h @ w_in
    q_a, kv_a, k_rope, gate_b, gate_c, u = jnp.split(z, AB_SPLITS, axis=-1)
    q = (rms_norm(q_a, q_a_norm) @ w_q_b).reshape(bsz, s_len, MLA_HEADS, QK_HEAD)
    kv = (rms_norm(kv_a, kv_a_norm) @ w_kv_b).reshape(bsz, s_len, MLA_HEADS, QK_NOPE + V_HEAD)
    k_nope, v = kv[..., :QK_NOPE], kv[..., QK_NOPE:]
    k_r = jnp.broadcast_to(k_rope[:, :, None, :], (bsz, s_len, MLA_HEADS, QK_ROPE))
    k = jnp.concatenate([k_nope, k_r], axis=-1)
    q = rms_norm(q, q_norm)
    k = rms_norm(k, k_norm)
    cos, sin = rope_tables(s_len, QK_ROPE)
    q = jnp.concatenate([q[..., :QK_NOPE], apply_rope(q[..., QK_NOPE:], cos, sin)], axis=-1)
    k = jnp.concatenate([k[..., :QK_NOPE], apply_rope(k[..., QK_NOPE:], cos, sin)], axis=-1)
    attn = dense_attention(q, k, v, QK_HEAD ** -0.5).reshape(bsz, s_len, MLA_WIDTH)
    cu = jnp.pad(gate_c * u, ((0, 0), (1, 1), (0, 0)))
    conv = cu[:, :-2] * conv_w[0] + cu[:, 1:-1] * conv_w[1] + cu[:, 2:] * conv_w[2]
    conv_out = gate_b * conv
    return jnp.concatenate([attn, conv_out], axis=-1) @ w_out


def window_gqa_mixer(h, w_in, q_norm, k_norm, sink, w_out):
    bsz, s_len, _ = h.shape
    nb = s_len // BLOCK
    z = h @ w_in
    q, k, v = jnp.split(z, [C_HEADS * C_HEAD_DIM, (C_HEADS + C_KV_HEADS) * C_HEAD_DIM], axis=-1)
    q = rms_norm(q.reshape(bsz, s_len, C_HEADS, C_HEAD_DIM), q_norm)
    k = rms_norm(k.reshape(bsz, s_len, C_KV_HEADS, C_HEAD_DIM), k_norm)
    v = v.reshape(bsz, s_len, C_KV_HEADS, C_HEAD_DIM)
    cos, sin = rope_tables(s_len, C_HEAD_DIM)
    q = apply_rope(q, cos, sin)
    k = apply_rope(k, cos, sin)
    qb = q.reshape(bsz, nb, BLOCK, C_KV_HEADS, C_GROUP, C_HEAD_DIM).transpose(1, 0, 2, 3, 4, 5)

    def band(t):
        tp = jnp.pad(t, ((0, 0), (BLOCK, BLOCK), (0, 0), (0, 0)))
        tp = tp.reshape(bsz, nb + 2, BLOCK, C_KV_HEADS, C_HEAD_DIM)
        tb = jnp.concatenate([tp[:, :-2], tp[:, 1:-1], tp[:, 2:]], axis=2)
        return tb.transpose(1, 0, 2, 3, 4)

    kb, vb = band(k), band(v)
    blk = jnp.arange(nb)[:, None, None] * BLOCK
    qpos = blk + jnp.arange(BLOCK)[None, :, None]
    kpos = blk - BLOCK + jnp.arange(3 * BLOCK)[None, None, :]
    valid = (jnp.abs(qpos - kpos) <= WINDOW) & (kpos >= 0) & (kpos < s_len)
    sink_f = sink.astype(jnp.float32).reshape(C_KV_HEADS, C_GROUP)

    def one_block(args):
        qblk, kblk, vblk, msk = args
        s = jnp.einsum('bqhgd,bkhd->bhgqk', qblk, kblk).astype(jnp.float32) * (C_HEAD_DIM ** -0.5)
        s = jnp.where(msk[None, None, None], s, NEG)
        sink_col = jnp.broadcast_to(sink_f[None, :, :, None, None], s.shape[:-1] + (1,))
        p = jax.nn.softmax(jnp.concatenate([s, sink_col], axis=-1), axis=-1)[..., :-1]
        return jnp.einsum('bhgqk,bkhd->bqhgd', p.astype(vblk.dtype), vblk)

    out = lax.map(one_block, (qb, kb, vb, valid))
    out = out.transpose(1, 0, 2, 3, 4, 5).reshape(bsz, s_len, C_HEADS * C_HEAD_DIM)
    return out @ w_out


def trunk(x, ffn_norm, ffn_w_gate, ffn_w_up, ffn_w_down, mix_norm,
          ab_w_in, ab_q_a_norm, ab_w_q_b, ab_kv_a_norm, ab_w_kv_b, ab_q_norm, ab_k_norm,
          ab_conv_w, ab_w_out, c_w_in, c_q_norm, c_k_norm, c_sink, c_w_out):
    for layer in range(DEPTH):
        x = x + 0.5 * swiglu(rms_norm(x, ffn_norm[layer, 0]), ffn_w_gate[layer, 0],
                             ffn_w_up[layer, 0], ffn_w_down[layer, 0])
        h = rms_norm(x, mix_norm[layer])
        i = layer // 2
        if layer % 2 == 0:
            x = x + mla_conv_mixer(h, ab_w_in[i], ab_q_a_norm[i], ab_w_q_b[i], ab_kv_a_norm[i],
                                   ab_w_kv_b[i], ab_q_norm[i], ab_k_norm[i], ab_conv_w[i], ab_w_out[i])
        else:
            x = x + window_gqa_mixer(h, c_w_in[i], c_q_norm[i], c_k_norm[i], c_sink[i], c_w_out[i])
        x = x + 0.5 * swiglu(rms_norm(x, ffn_norm[layer, 1]), ffn_w_gate[layer, 1],
                             ffn_w_up[layer, 1], ffn_w_down[layer, 1])
    return x


def setup_inputs(seed: int = 0) -> dict:
    key = jax.random.key(seed)
    ks = jax.random.split(key, 24)
    f32 = jnp.float32

    def w(k, shape, fan_in):
        return jax.random.normal(k, shape, f32) * (fan_in ** -0.5)

    def gain(k, shape):
        return 1.0 + 0.01 * jax.random.normal(k, shape, f32)

    return {
        "x_prompt": jax.random.normal(ks[0], (BATCH, SEQ, D_MODEL), f32),
        "x_sample": jax.random.normal(ks[1], (DEC_BATCH, DEC_SEQ, D_MODEL), f32),
        "ffn_norm": gain(ks[2], (DEPTH, 2, D_MODEL)),
        "ffn_w_gate": w(ks[3], (DEPTH, 2, D_MODEL, D_FF), D_MODEL),
        "ffn_w_up": w(ks[4], (DEPTH, 2, D_MODEL, D_FF), D_MODEL),
        "ffn_w_down": w(ks[5], (DEPTH, 2, D_FF, D_MODEL), D_FF),
        "mix_norm": gain(ks[6], (DEPTH, D_MODEL)),
        "ab_w_in": w(ks[7], (N_AB_LAYERS, D_MODEL, AB_IN), D_MODEL),
        "ab_q_a_norm": gain(ks[8], (N_AB_LAYERS, Q_LORA)),
        "ab_w_q_b": w(ks[9], (N_AB_LAYERS, Q_LORA, MLA_HEADS * QK_HEAD), Q_LORA),
        "ab_kv_a_norm": gain(ks[10], (N_AB_LAYERS, KV_LORA)),
        "ab_w_kv_b": w(ks[11], (N_AB_LAYERS, KV_LORA, MLA_HEADS * (QK_NOPE + V_HEAD)), KV_LORA),
        "ab_q_norm": gain(ks[12], (N_AB_LAYERS, QK_HEAD)),
        "ab_k_norm": gain(ks[13], (N_AB_LAYERS, QK_HEAD)),
        "ab_conv_w": w(ks[14], (N_AB_LAYERS, CONV_K, CONV_WIDTH), CONV_K),
        "ab_w_out": w(ks[15], (N_AB_LAYERS, D_MODEL, D_MODEL), D_MODEL),
        "c_w_in": w(ks[16], (N_C_LAYERS, D_MODEL, C_IN), D_MODEL),
        "c_q_norm": gain(ks[17], (N_C_LAYERS, C_HEAD_DIM)),
        "c_k_norm": gain(ks[18], (N_C_LAYERS, C_HEAD_DIM)),
        "c_sink": jax.random.normal(ks[19], (N_C_LAYERS, C_HEADS), f32),
        "c_w_out": w(ks[20], (N_C_LAYERS, C_HEADS * C_HEAD_DIM, D_MODEL), C_HEADS * C_HEAD_DIM),
    }


def reference(x_prompt, x_sample, ffn_norm, ffn_w_gate, ffn_w_up, ffn_w_down, mix_norm,
              ab_w_in, ab_q_a_norm, ab_w_q_b, ab_kv_a_norm, ab_w_kv_b, ab_q_norm, ab_k_norm,
              ab_conv_w, ab_w_out, c_w_in, c_q_norm, c_k_norm, c_sink, c_w_out):
    y_prompt = trunk(x_prompt, ffn_norm, ffn_w_gate, ffn_w_up, ffn_w_down, mix_norm,
                     ab_w_in, ab_q_a_norm, ab_w_q_b, ab_kv_a_norm, ab_w_kv_b, ab_q_norm, ab_k_norm,
                     ab_conv_w, ab_w_out, c_w_in, c_q_norm, c_k_norm, c_sink, c_w_out)
    y_sample = trunk(x_sample, ffn_norm, ffn_w_gate, ffn_w_up, ffn_w_down, mix_norm,
                     ab_w_in, ab_q_a_norm, ab_w_q_b, ab_kv_a_norm, ab_w_kv_b, ab_q_norm, ab_k_norm,
                     ab_conv_w, ab_w_out, c_w_in, c_q_norm, c_k_norm, c_sink, c_w_out)
    return (y_prompt, y_sample)
```

```python
import math
from contextlib import ExitStack

import numpy as np
import concourse.bass as bass
import concourse.mybir as mybir
from concourse.bass_utils import run_bass_kernel_spmd

F32 = mybir.dt.float32
BF16 = mybir.dt.bfloat16
AF = mybir.ActivationFunctionType
ALU = mybir.AluOpType

import os
DBG_STOP = int(os.environ.get('DBG_STOP', '0'))
DBG_OUT = int(os.environ.get('DBG_OUT', '0'))
P = 128
T = 1024
HT = 512
D = 2048
DC = 16
DFF = 5632
FC = 44
EPS = 1e-6
S_FULL = 4096
NH = 8
QL = 512
KVL = 512
NOPE = 128
ROPE = 64
QKH = 192
ABIN = 4160
CH = 16
CKV = 4
CIN = 3072


def I(name, *args, **kw):
    return [(name, args, kw)]


class Rec:
    def __init__(self):
        self.ops = []

    def __getattr__(self, name):
        def f(*args, **kw):
            self.ops.append((name, args, kw))
            return None
        return f


class Buf:
    __slots__ = ("name", "lastw", "readers", "sem", "dma_count", "excl")

    def __init__(self, name, excl=False):
        self.name = name
        self.excl = excl
        self.lastw = None
        self.readers = {}
        self.sem = None
        self.dma_count = 0


class Op:
    __slots__ = ("eng", "fn", "deps", "count", "dma_buf", "semval", "need_inc")


class Sched:
    ENGS = ("pe", "act", "dve", "pool", "sp")

    def __init__(self):
        self.ops = {e: [] for e in self.ENGS}
        self.dma_bufs = []

    def add(self, eng, fn, reads=(), writes=(), dma_buf=None):
        if callable(fn):
            rec = Rec()
            fn(rec)
            fn = rec.ops
        op = Op()
        op.eng = eng
        op.fn = fn
        op.count = 0
        op.need_inc = False
        op.dma_buf = dma_buf
        op.semval = 0
        deps = set()
        xr = [b for b in reads if b.excl]
        if xr:
            reads = [b for b in reads if not b.excl]
            writes = list(writes) + [b for b in xr if b not in writes]
        for b in reads:
            if b.lastw is not None:
                deps.add(b.lastw)
        for b in writes:
            if b.lastw is not None:
                deps.add(b.lastw)
            deps.update(b.readers.values())
        deps.discard(op)
        op.deps = deps
        if dma_buf is not None:
            if dma_buf.dma_count == 0:
                self.dma_bufs.append(dma_buf)
            dma_buf.dma_count += 1
            op.semval = 16 * dma_buf.dma_count
            key = ("dma", id(dma_buf))
        else:
            key = eng
        for b in reads:
            b.readers[key] = op
        for b in writes:
            b.lastw = op
            b.readers = {}
        self.ops[eng].append(op)
        return op

    def emit(self, nc, stack):
        engsem = {}
        for e in ("pe", "act", "dve", "pool"):
            engsem[e] = stack.enter_context(nc.semaphore("s_" + e))
        for i, b in enumerate(self.dma_bufs):
            b.sem = stack.enter_context(nc.semaphore("d%d" % i))
        for e in self.ENGS:
            for op in self.ops[e]:
                for d in op.deps:
                    if d.dma_buf is None:
                        d.need_inc = True
        for e in ("pe", "act", "dve", "pool"):
            c = 0
            for op in self.ops[e]:
                if op.dma_buf is None and op.need_inc:
                    c += 1
                    op.count = c
        block = stack.enter_context(nc.Block())
        engobj = {"pe": "tensor", "act": "scalar", "dve": "vector", "pool": "gpsimd", "sp": "sync"}

        def make(e):
            ops = self.ops[e]

            def body(eng):
                waited = {}
                for op in ops:
                    need = {}
                    for d in op.deps:
                        if d.dma_buf is not None:
                            s, v = d.dma_buf.sem, d.semval
                        else:
                            if d.eng == "pe" and e == "pe":
                                continue
                            s, v = engsem[d.eng], d.count
                        k = id(s)
                        if v > need.get(k, (None, 0))[1]:
                            need[k] = (s, v)
                    for k, (s, v) in need.items():
                        if v > waited.get(k, 0):
                            eng.wait_ge(s, v)
                            waited[k] = v
                    inst = None
                    for (name, args, kw) in op.fn:
                        inst = getattr(eng, name)(*args, **kw)
                    if op.dma_buf is not None:
                        inst.then_inc(op.dma_buf.sem, 16)
                    elif op.need_inc:
                        inst.then_inc(engsem[e], 1)
                if e in ("sp", "pool"):
                    last = {}
                    for op in ops:
                        if op.dma_buf is not None:
                            last[id(op.dma_buf)] = (op.dma_buf.sem, max(op.semval, last.get(id(op.dma_buf), (None, 0))[1]))
                    for k, (s, v) in last.items():
                        if v > waited.get(k, 0):
                            eng.wait_ge(s, v)
            return body

        for e in self.ENGS:
            getattr(block, engobj[e])(make(e))


class Builder:
    def __init__(self, nseq, S, stages="all"):
        self.nseq = nseq
        self.S = S
        self.NT = S // T
        self.stages = stages
        self.sc = Sched()
        self.nc = bass.Bass("TRN2", target_bir_lowering=False)
        self.stack = ExitStack()
        self._rr = {}

    def dram_in(self, name, shape, dt=F32):
        return self.nc.dram_tensor(name, list(shape), dt, kind="ExternalInput").ap()

    def sb(self, name, shape, dt):
        return self.stack.enter_context(self.nc.sbuf_tensor(name, list(shape), dt))

    def rr(self, key, n):
        i = self._rr.get(key, 0)
        self._rr[key] = i + 1
        return i % n

    def dma(self, q, out, in_, reads, writes, dma_buf, slow=False):
        self.sc.add(q, I("dma_start", out=out, in_=in_, allow_slow_non_contiguous=slow),
                    reads=reads, writes=writes, dma_buf=dma_buf)

    def declare(self):
        nc, S, nseq = self.nc, self.S, self.nseq
        self.x_in = self.dram_in("x", [nseq, S, D])
        self.ffn_norm = self.dram_in("ffn_norm", [2, 2, D])
        self.w_gate = self.dram_in("ffn_w_gate", [2, 2, D, DFF])
        self.w_up = self.dram_in("ffn_w_up", [2, 2, D, DFF])
        self.w_down = self.dram_in("ffn_w_down", [2, 2, DFF, D])
        self.mix_norm = self.dram_in("mix_norm", [2, D])
        self.ab_w_in = self.dram_in("ab_w_in", [1, D, ABIN])
        self.ab_q_a_norm = self.dram_in("ab_q_a_norm", [1, QL])
        self.ab_w_q_b = self.dram_in("ab_w_q_b", [1, QL, NH * QKH])
        self.ab_kv_a_norm = self.dram_in("ab_kv_a_norm", [1, KVL])
        self.ab_w_kv_b = self.dram_in("ab_w_kv_b", [1, KVL, NH * 256])
        self.ab_q_norm = self.dram_in("ab_q_norm", [1, QKH])
        self.ab_k_norm = self.dram_in("ab_k_norm", [1, QKH])
        self.ab_conv_w = self.dram_in("ab_conv_w", [1, 3, 1024])
        self.ab_w_out = self.dram_in("ab_w_out", [1, D, D])
        self.c_w_in = self.dram_in("c_w_in", [1, D, CIN])
        self.c_q_norm = self.dram_in("c_q_norm", [1, 128])
        self.c_k_norm = self.dram_in("c_k_norm", [1, 128])
        self.c_sink = self.dram_in("c_sink", [1, CH])
        self.c_w_out = self.dram_in("c_w_out", [1, D, D])
        self.rope64 = self.dram_in("rope64", [2, 128, S])
        self.rope128 = self.dram_in("rope128", [2, 128, S])
        self.masks = self.dram_in("masks", [2, 128, 128])
        self.y_out = nc.dram_tensor("y", [nseq, S, D], F32, kind="ExternalOutput").ap()
        self.XT = nc.dram_tensor("XT", [nseq, P, DC, S], F32, kind="Internal").ap()
        self.XT_buf = [[[Buf("XT%d_%d_%d" % (s, t, c)) for c in range(DC)] for t in range(self.NT)] for s in range(nseq)]
        self.XN = self.sb("XN", [P, DC, T], BF16)
        self.XN_buf = [Buf("XN%d" % c) for c in range(DC)]
        self.HID = self.sb("HID", [P, FC, T], BF16)
        self.HID_buf = [Buf("HID%d" % c) for c in range(FC)]
        self.WA = [self.sb("WA%d" % i, [P, DC, 256], BF16) for i in range(4)]
        self.WA_buf = [Buf("WA%d" % i) for i in range(4)]
        self.WB = [self.sb("WB%d" % i, [P, FC, 128], BF16) for i in range(2)]
        self.WB_buf = [Buf("WB%d" % i) for i in range(2)]
        self.XCH = [self.sb("XCH%d" % i, [P, T], F32) for i in range(4)]
        self.XCH_buf = [Buf("XCH%d" % i) for i in range(4)]
        self.SQ = self.sb("SQ", [P, T], BF16)
        self.SQ_buf = Buf("SQ")
        self.SL = self.sb("SL", [P, T], F32)
        self.SL_buf = Buf("SL")
        self.RS = self.sb("RS", [P, T], F32)
        self.RS_buf = Buf("RS")
        self.ones_bf = self.sb("ones_bf", [P, P], BF16)
        self.ident = self.sb("ident", [P, P], F32)
        self.const_buf = Buf("const")
        self.gains = self.sb("gains", [P, 6, DC], F32)
        self.epsT = self.sb("epsT", [P, 1], F32)
        self.ident_in = self.dram_in("ident_in", [P, P])
        self.PS = [self.stack.enter_context(nc.psum_tensor("PS%d" % i, [P, T], F32)) for i in range(4)]
        self.PSB = [Buf("PSB%d" % i, excl=True) for i in range(8)]
        self.PS_buf = [[self.PSB[2 * i], self.PSB[2 * i + 1]] for i in range(4)]
        self.declare2()

    def load_consts(self):
        sc = self.sc
        cb = [self.const_buf]
        sc.add("dve", I("memset", self.ones_bf[:], 1.0), writes=cb)
        sc.add("dve", I("memset", self.epsT[:], EPS), writes=cb)
        self.cdma = Buf("cdma")
        self.dma("sp", self.ident[:], self.ident_in, [], cb, self.cdma)
        for l in range(2):
            for j in range(2):
                self.dma("sp", self.gains[:, l * 2 + j, :], self.ffn_norm[l, j].rearrange("(c p) -> p c", p=P), [], cb, self.cdma, slow=True)
            self.dma("sp", self.gains[:, 4 + l, :], self.mix_norm[l].rearrange("(c p) -> p c", p=P), [], cb, self.cdma, slow=True)

    def ps_next(self):
        i = self.rr("ps", 3)
        return self.PS[i], self.PS_buf[i]

    @property
    def PST(self):
        return self.PS[3], self.PS_buf[3]

    def mm_group(self, ps, ps_buf, lhs_fn, rhs_fn, K, reads, width=T):
        nh = width // HT

        def fn(e):
            inst = None
            for k in range(K):
                for h in range(nh):
                    inst = e.matmul(ps[:, h * HT:(h + 1) * HT], lhs_fn(k), rhs_fn(k, h), start=(k == 0), stop=(k == K - 1))
            return inst
        self.sc.add("pe", fn, reads=list(reads) + [self.const_buf], writes=ps_buf)

    def stats_accum(self, src_ap, src_buf, c, nchunks=DC):
        sc = self.sc
        SQ, SQb = self.SQ, self.SQ_buf
        sc.add("act", I("activation", out=SQ[:], in_=src_ap, func=AF.Square), reads=src_buf, writes=[SQb])
        pst, pstb = self.PST

        def fn(e):
            inst = None
            for h in range(2):
                inst = e.matmul(pst[:, h * HT:(h + 1) * HT], self.ones_bf[:], SQ[:, h * HT:(h + 1) * HT],
                                start=(c == 0), stop=(c == nchunks - 1))
            return inst
        sc.add("pe", fn, reads=[SQb, self.const_buf], writes=pstb)

    def stats_final(self, n, out=None, outb=None):
        sc = self.sc
        pst, pstb = self.PST
        RS = self.RS[:] if out is None else out
        RSb = [self.RS_buf] if outb is None else outb
        sc.add("act", I("activation", out=RS, in_=pst[:], func=AF.Sqrt, scale=1.0 / n, bias=self.epsT[:]),
               reads=pstb + [self.const_buf], writes=RSb)
        sc.add("dve", I("reciprocal", out=RS, in_=RS), reads=RSb, writes=RSb)

    def xchunk_out(self, s, t, c, xi, do_stats=True):
        XT_ap = self.XT[s, :, c, t * T:(t + 1) * T]
        self.dma("sp", XT_ap, self.XCH[xi][:], [self.XCH_buf[xi]], [self.XT_buf[s][t][c]], self.XCH_buf[xi])
        if do_stats:
            self.stats_accum(self.XCH[xi][:], [self.XCH_buf[xi]], c)
            if c == DC - 1:
                self.stats_final(float(D))

    def resid_step(self, s, t, c, ps, ps_buf, scale, do_stats=True):
        xi = self.rr("xch", 4)
        XT_ap = self.XT[s, :, c, t * T:(t + 1) * T]
        X = self.XCH[xi]
        self.dma("sp", X[:], XT_ap, [self.XT_buf[s][t][c]], [self.XCH_buf[xi]], self.XCH_buf[xi])
        self.sc.add("dve", I("scalar_tensor_tensor", out=X[:], in0=ps[:], scalar=scale, in1=X[:], op0=ALU.mult, op1=ALU.add),
                    reads=ps_buf + [self.XCH_buf[xi]], writes=[self.XCH_buf[xi]])
        self.xchunk_out(s, t, c, xi, do_stats)

    def normalize(self, s, t, gidx):
        for c in range(DC):
            xi = self.rr("xch", 4)
            X = self.XCH[xi]
            XT_ap = self.XT[s, :, c, t * T:(t + 1) * T]
            self.dma("sp", X[:], XT_ap, [self.XT_buf[s][t][c]], [self.XCH_buf[xi]], self.XCH_buf[xi])
            g = self.gains[:, gidx, c:c + 1]
            self.sc.add("dve", I("scalar_tensor_tensor", out=self.XN[:, c, :], in0=X[:], scalar=g, in1=self.RS[:],
                                                                               op0=ALU.mult, op1=ALU.mult),
                        reads=[self.XCH_buf[xi], self.RS_buf, self.const_buf], writes=[self.XN_buf[c]])

    def load_wa(self, w2d, c0, ncols=256, K=DC):
        i = self.rr("wa", 4)
        src = w2d.rearrange("(c p) f -> p c f", p=P)[:, :, c0:c0 + ncols]
        self.dma("pool", self.WA[i][:, :K, :ncols], src, [], [self.WA_buf[i]], self.WA_buf[i])
        return self.WA[i], self.WA_buf[i]

    def load_wb(self, w2d, c0):
        i = self.rr("wb", 2)
        src = w2d.rearrange("(c p) f -> p c f", p=P)
        for a, b in ((0, 16), (16, 32), (32, FC)):
            self.dma("pool", self.WB[i][:, a:b, :], src[:, a:b, c0:c0 + 128], [], [self.WB_buf[i]], self.WB_buf[i])
        return self.WB[i], self.WB_buf[i]

    def tr_in(self, s, t):
        nc, sc = self.nc, self.sc
        XR = self.HID[:].rearrange("p c t -> p (c t)")[:, 0:32 * T].bitcast(F32).rearrange("p (b f) -> p b f", b=8)
        for b in range(8):
            bufs = self.HID_buf[b * 4:(b + 1) * 4]
            self.dma("sp", XR[:, b, :], self.x_in[s, t * T + b * P: t * T + (b + 1) * P, :], [], bufs, bufs[0])
        for c in range(DC):
            ps, psb = self.ps_next()

            def fn(e, ps=ps, c=c):
                inst = None
                for b in range(8):
                    inst = e.transpose(out=ps[:, b * P:(b + 1) * P], in_=XR[:, b, c * P:(c + 1) * P], identity=self.ident[:])
                return inst
            sc.add("pe", fn, reads=self.HID_buf[0:32] + [self.const_buf], writes=psb)
            xi = self.rr("xch", 4)
            X = self.XCH[xi]
            sc.add("dve", I("tensor_copy", out=X[:], in_=ps[:]), reads=psb, writes=[self.XCH_buf[xi]])
            self.xchunk_out(s, t, c, xi)

    def tr_out(self, s, t):
        sc = self.sc
        XF = self.HID[:].rearrange("p c t -> p (c t)")[:, 0:32 * T].bitcast(F32).rearrange("p (c f) -> p c f", c=DC)
        for c in range(DC):
            bufs = self.HID_buf[c * 2:(c + 1) * 2]
            self.dma("sp", XF[:, c, :], self.XT[s, :, c, t * T:(t + 1) * T], [self.XT_buf[s][t][c]], bufs, bufs[0])
        OUT = self.XN[:].rearrange("p c t -> p (c t)").bitcast(F32).rearrange("p (o f) -> p o f", o=4)
        for b in range(8):
            oi = self.rr("outst", 4)
            obufs = self.XN_buf[oi * 4:(oi + 1) * 4]
            for h in range(2):
                ps, psb = self.ps_next()

                def fn(e, ps=ps, h=h, b=b):
                    inst = None
                    for i in range(8):
                        c = h * 8 + i
                        inst = e.transpose(out=ps[:, i * P:(i + 1) * P], in_=XF[:, c, b * P:(b + 1) * P], identity=self.ident[:])
                    return inst
                sc.add("pe", fn, reads=self.HID_buf[0:32] + [self.const_buf], writes=psb)
                eng = "dve" if h == 0 else "act"
                if eng == "dve":
                    sc.add("dve", I("tensor_copy", out=OUT[:, oi, h * T:(h + 1) * T], in_=ps[:]),
                           reads=psb, writes=obufs[h * 2:(h + 1) * 2])
                else:
                    sc.add("act", I("activation", out=OUT[:, oi, h * T:(h + 1) * T], in_=ps[:], func=AF.Copy),
                           reads=psb, writes=obufs[h * 2:(h + 1) * 2])
            self.dma("sp", self.y_out[s, t * T + b * P: t * T + (b + 1) * P, :], OUT[:, oi, :], obufs, [], obufs[0])

    def ffn(self, s, t, l, j, do_stats=True):
        sc = self.sc
        self.normalize(s, t, l * 2 + j)
        wg2d, wu2d, wd2d = self.w_gate[l, j], self.w_up[l, j], self.w_down[l, j]
        NF2 = FC // 2
        loads = [None] * (NF2 + 1)
        loads[0] = (self.load_wa(wg2d, 0), self.load_wa(wu2d, 0))
        for f2 in range(NF2):
            if f2 + 1 < NF2:
                loads[f2 + 1] = (self.load_wa(wg2d, (f2 + 1) * 256), self.load_wa(wu2d, (f2 + 1) * 256))
            (wg, wgb), (wu, wub) = loads[f2]
            for sub in range(2):
                fc = f2 * 2 + sub
                pg, pgb = self.ps_next()
                self.mm_group(pg, pgb, lambda k, wg=wg, sub=sub: wg[:, k, sub * P:(sub + 1) * P],
                              lambda k, h: self.XN[:, k, h * HT:(h + 1) * HT], DC, [wgb] + self.XN_buf)
                pu, pub = self.ps_next()
                self.mm_group(pu, pub, lambda k, wu=wu, sub=sub: wu[:, k, sub * P:(sub + 1) * P],
                              lambda k, h: self.XN[:, k, h * HT:(h + 1) * HT], DC, [wub] + self.XN_buf)
                sc.add("act", I("activation", out=self.SL[:], in_=pg[:], func=AF.Silu), reads=pgb, writes=[self.SL_buf])
                sc.add("dve", I("tensor_tensor", out=self.HID[:, fc, :], in0=pu[:], in1=self.SL[:], op=ALU.mult),
                       reads=pub + [self.SL_buf], writes=[self.HID_buf[fc]])
        nxt = self.load_wb(wd2d, 0)
        for dc in range(DC):
            wd, wdb = nxt
            if dc + 1 < DC:
                nxt = self.load_wb(wd2d, (dc + 1) * P)
            py, pyb = self.ps_next()
            self.mm_group(py, pyb, lambda k, wd=wd: wd[:, k, :], lambda k, h: self.HID[:, k, h * HT:(h + 1) * HT], FC,
                          [wdb] + self.HID_buf)
            self.resid_step(s, t, dc, py, pyb, 0.5, do_stats)

    def declare2(self):
        nc, S, nseq, NT = self.nc, self.S, self.nseq, self.NT

        def scratch(name, shape, dt):
            return nc.dram_tensor(name, list(shape), dt, kind="ExternalOutput" if DBG_OUT else "Internal").ap()

        def bufs(name, n1):
            return [[[Buf("%s%d_%d_%d" % (name, s, i, t)) for t in range(NT)] for i in range(n1)] for s in range(nseq)]
        self.QAs = scratch("QAs", [nseq, P, 8, S], BF16); self.QA_b = bufs("QA", 8)
        self.QRs = scratch("QRs", [nseq, P, 4, S], BF16); self.QR_b = bufs("QR", 4)
        self.KAs = scratch("KAs", [nseq, P, 8, S], BF16); self.KA_b = bufs("KA", 8)
        self.KRs = scratch("KRs", [nseq, P, 4, S], BF16); self.KR_b = bufs("KR", 4)
        self.Vs = scratch("Vs", [nseq, S, 1024], BF16); self.V_b = bufs("V", 1)
        self.CUs = scratch("CUs", [nseq, P, 8, S], F32); self.CU_b = bufs("CU", 8)
        self.GBs = scratch("GBs", [nseq, P, 8, S], F32); self.GB_b = bufs("GB", 8)
        self.ATs = scratch("ATs", [nseq, P, 8, S], BF16); self.AT_b = bufs("AT", 8)
        self.Q1s = scratch("Q1s", [nseq, P, 16, S], BF16); self.Q1_b = bufs("Q1", 16)
        self.K1s = scratch("K1s", [nseq, P, 4, S], BF16); self.K1_b = bufs("K1", 4)
        self.V1s = scratch("V1s", [nseq, S, 512], BF16); self.V1_b = bufs("V1", 1)
        self.A1s = scratch("A1s", [nseq, P, 16, S], BF16); self.A1_b = bufs("A1", 4)
        self.gqa = self.sb("gqa", [P, 4], F32)
        self.gkva = self.sb("gkva", [P, 4], F32)
        self.gvec = self.sb("gvec", [P, 10], F32)
        self.convw = self.sb("convw", [P, 8, 3], F32)
        self.esink = self.sb("esink", [P, 16], F32)
        self.sink1 = self.sb("sink1", [1, 16], F32)
        self.ones_f = self.sb("ones_f", [1, P], F32)
        self.zeros_f = self.sb("zeros_f", [P, P], F32)
        self.ehalf = self.sb("ehalf", [P, 2, P], BF16)
        self.MASK = self.sb("MASK", [P, 2, 4, P], BF16)
        self.RKVT = self.sb("RKVT", [P, 16], F32)
        self.RKVT_buf = Buf("RKVT")
        self.HIDflat = self.HID[:].rearrange("p c t -> p (c t)")
        self.PT_bufs = [Buf("PT%d" % i) for i in range(4)]
        self.AO_bufs = [Buf("AO%d" % i) for i in range(2)]
        self.VO_bufs = [Buf("VO%d" % i) for i in range(2)]
        self.PT1_bufs = [Buf("PT1_%d" % i) for i in range(6)]
        self.AO1_bufs = [Buf("AO1_%d" % i) for i in range(2)]

    def arena(self, chunk0, nchunks, dt):
        e0 = chunk0 * 1024
        ap = self.HIDflat[:, e0:e0 + nchunks * 1024]
        if dt == F32:
            ap = ap.bitcast(F32)
        return ap, self.HID_buf[chunk0:chunk0 + nchunks]

    def fence(self, chunk0, n):
        ap, b = self.arena(chunk0, n, BF16)
        self.sc.add("pool", I("memset", ap[:, 0:2], 0.0), writes=b)

    def waflat(self, i):
        return self.WA[i][:].rearrange("p c f -> p (c f)")

    def load_consts2(self):
        sc = self.sc
        cb = [self.const_buf]
        cd = self.cdma
        sc.add("dve", I("memset", self.ones_f[:], 1.0), writes=cb)
        sc.add("dve", I("memset", self.zeros_f[:], 0.0), writes=cb)
        sc.add("dve", I("memset", self.ehalf[:], 0.0), writes=cb)
        sc.add("dve", I("memset", self.ehalf[0:64, 0, :], 1.0), writes=cb)
        sc.add("dve", I("memset", self.ehalf[64:128, 1, :], 1.0), writes=cb)
        self.dma("sp", self.gqa[:], self.ab_q_a_norm[0].rearrange("(c p) -> p c", p=P), [], cb, cd, slow=True)
        self.dma("sp", self.gkva[:], self.ab_kv_a_norm[0].rearrange("(c p) -> p c", p=P), [], cb, cd, slow=True)

        def col(v, a, b):
            return v[a:b].rearrange("(p o) -> p o", o=1)
        gv = self.gvec
        for base, vec in ((0, self.ab_q_norm[0]), (3, self.ab_k_norm[0])):
            self.dma("sp", gv[:, base:base + 1], col(vec, 0, 128), [], cb, cd)
            for r in range(2):
                self.dma("sp", gv[r * 64:(r + 1) * 64, base + 1:base + 2], col(vec, 128, 192), [], cb, cd)
                self.dma("sp", gv[r * 64:r * 64 + 32, base + 2:base + 3], col(vec, 160, 192), [], cb, cd)
                self.dma("sp", gv[r * 64 + 32:r * 64 + 64, base + 2:base + 3], col(vec, 128, 160), [], cb, cd)
        for base, vec in ((6, self.c_q_norm[0]), (8, self.c_k_norm[0])):
            self.dma("sp", gv[:, base:base + 1], col(vec, 0, 128), [], cb, cd)
            self.dma("sp", gv[0:64, base + 1:base + 2], col(vec, 64, 128), [], cb, cd)
            self.dma("sp", gv[64:128, base + 1:base + 2], col(vec, 0, 64), [], cb, cd)
        for kk in range(3):
            self.dma("sp", self.convw[:, :, kk], self.ab_conv_w[0, kk].rearrange("(c p) -> p c", p=P), [], cb, cd, slow=True)
        self.dma("sp", self.sink1[:], self.c_sink, [], cb, cd)
        mb = Buf("maskdma")
        for w in range(2):
            for r in range(4):
                self.dma("pool", self.MASK[:, w, r, :], self.masks[w], [], cb, mb)
        ps, psb = self.ps_next()
        sc.add("pe", I("matmul", ps[:, 0:16], self.ones_f[0:1, :], self.sink1[0:1, :], start=True, stop=True), reads=cb, writes=psb)
        sc.add("act", I("activation", out=self.esink[:], in_=ps[:, 0:16], func=AF.Exp), reads=psb, writes=cb)

    def pipeline(self, steps, ahead=1):
        handles = {}
        n = len(steps)
        for i in range(min(ahead, n)):
            handles[i] = steps[i][0]()
        for i in range(n):
            if i + ahead < n:
                handles[i + ahead] = steps[i + ahead][0]()
            steps[i][1](handles.pop(i))

    def xn_rhs(self, k, h):
        return self.XN[:, k, h * HT:(h + 1) * HT]

    def pool_tt(self, out, in0, in1, op, reads, writes):
        self.sc.add("pool", I("tensor_tensor", out=out, in0=in0, in1=in1, op=op), reads=reads, writes=writes)

    def dve_tt(self, out, in0, in1, op, reads, writes):
        self.sc.add("dve", I("tensor_tensor", out=out, in0=in0, in1=in1, op=op), reads=reads, writes=writes)

    def rope_tables(self, t, table, c0, gidx):
        tsl = slice(t * T, (t + 1) * T)
        TC, TCb = self.arena(c0, 2, F32)
        TS, TSb = self.arena(c0 + 2, 2, F32)
        self.dma("sp", TC, table[0][:, tsl], [], TCb, TCb[0])
        self.dma("sp", TS, table[1][:, tsl], [], TSb, TSb[0])
        out = []
        for i, (src, srcb, gi) in enumerate(((TC, TCb, gidx[0]), (TS, TSb, gidx[1]), (TC, TCb, gidx[2]), (TS, TSb, gidx[3]))):
            dst, dstb = self.arena(c0 + 4 + 2 * i, 2, F32)
            g = self.gvec[:, gi:gi + 1]
            self.sc.add("pool", I("tensor_scalar", out=dst, in0=src, scalar1=g, scalar2=None, op0=ALU.mult),
                        reads=srcb + [self.const_buf], writes=dstb)
            out.append((dst, dstb))
        return out

    def head_stats_chain(self, pst, pstb, R2, R2b, addt, Rmul, Rmulb):
        sc = self.sc
        SL, SLb, RS, RSb = self.SL, [self.SL_buf], self.RS, [self.RS_buf]
        self.dve_tt(SL[:], pst[:], R2, ALU.mult, pstb + R2b, SLb)
        if addt is not None:
            self.pool_tt(SL[:], SL[:], addt[0], ALU.add, SLb + addt[1], SLb)
        sc.add("act", I("activation", out=SL[:], in_=SL[:], func=AF.Sqrt, bias=self.epsT[:]), reads=SLb + [self.const_buf], writes=SLb)
        sc.add("dve", I("reciprocal", out=SL[:], in_=SL[:]), reads=SLb, writes=SLb)
        self.pool_tt(RS[:], SL[:], Rmul, ALU.mult, SLb + Rmulb, RSb)

    def l0_pre(self, s, t):
        sc = self.sc
        A = self.arena
        cb = [self.const_buf]
        tsl = slice(t * T, (t + 1) * T)
        self.normalize(s, t, 4)
        if DBG_STOP == 11:
            return
        (CGq, CGqb), (SGq, SGqb), (CGk, CGkb), (SGk, SGkb) = self.rope_tables(t, self.rope64, 16, (1, 2, 4, 5))
        if DBG_STOP == 12:
            return
        w2d = self.ab_w_in[0]
        QAG, QAGb = A(0, 4, BF16); QAG = QAG.rearrange("p (c t) -> p c t", c=4)
        KVG, KVGb = A(4, 4, BF16); KVG = KVG.rearrange("p (c t) -> p c t", c=4)
        RQA, RQAb = A(8, 2, F32); RQA2, RQA2b = A(10, 2, F32)
        RKV, RKVb = A(12, 2, F32); RKV2, RKV2b = A(14, 2, F32)
        KRR, KRRb = A(28, 2, F32); SSKR, SSKRb = A(30, 2, F32)
        QRR, QRRb = A(32, 2, F32)
        SQR, SQRb = A(42, 1, BF16)
        R1, R1b, R2_, R2b_ = self.XCH[0], [self.XCH_buf[0]], self.XCH[1], [self.XCH_buf[1]]
        pst, pstb = self.PST
        SL, SLb, RS, RSb = self.SL, [self.SL_buf], self.RS, [self.RS_buf]

        for grp, (dst, dstb, gt, Rt, Rtb, R2t, R2tb) in enumerate(((QAG, QAGb, self.gqa, RQA, RQAb, RQA2, RQA2b),
                                                                   (KVG, KVGb, self.gkva, RKV, RKVb, RKV2, RKV2b))):
            steps = []
            for half in range(2):
                def load(grp=grp, half=half):
                    return self.load_wa(w2d, grp * 512 + half * 256)

                def comp(hd, half=half, dst=dst, dstb=dstb, gt=gt):
                    w, wb = hd
                    for sub in range(2):
                        j = half * 2 + sub
                        ps, psb = self.ps_next()
                        self.mm_group(ps, psb, lambda k, w=w, sub=sub: w[:, k, sub * P:(sub + 1) * P], self.xn_rhs, DC, [wb] + self.XN_buf)
                        if DBG_STOP != 131:
                            self.stats_accum(ps[:], psb, j, 4)
                        if DBG_STOP != 132:
                            sc.add("dve", I("tensor_scalar", out=dst[:, j, :], in0=ps[:], scalar1=gt[:, j:j + 1], scalar2=None, op0=ALU.mult),
                                   reads=psb + cb, writes=[dstb[j]])
                steps.append((load, comp))
            self.pipeline(steps)
            if DBG_STOP in (13, 131, 132):
                return
            self.stats_final(512.0, Rt, Rtb)
            if DBG_STOP == 14:
                return
            sc.add("dve", I("scalar_tensor_tensor", out=R2t, in0=Rt, scalar=1.0 / 192.0, in1=Rt, op0=ALU.mult, op1=ALU.mult),
                   reads=Rtb, writes=R2tb)

        if DBG_STOP == 1:
            return
        i = self.rr("wa", 4)
        wv = self.WA[i]
        src = w2d.rearrange("(c p) f -> p c f", p=P)
        for (d0, s0, n) in ((0, 1024, 64), (64, 1024, 64), (128, 1056, 32), (160, 1024, 32), (192, 1056, 32), (224, 1024, 32)):
            self.dma("pool", wv[:, :, d0:d0 + n], src[:, :, s0:s0 + n], [], [self.WA_buf[i]], self.WA_buf[i])
        pa, pab = self.ps_next()
        self.mm_group(pa, pab, lambda k: wv[:, k, 0:128], self.xn_rhs, DC, [self.WA_buf[i]] + self.XN_buf)
        pb, pbb = self.ps_next()
        self.mm_group(pb, pbb, lambda k: wv[:, k, 128:256], self.xn_rhs, DC, [self.WA_buf[i]] + self.XN_buf)
        self.dve_tt(R1[:], pa[:], CGk, ALU.mult, pab + CGkb, R1b)
        self.dve_tt(R2_[:], pb[:], SGk, ALU.mult, pbb + SGkb, R2b_)
        self.pool_tt(KRR, R1[:], R2_[:], ALU.add, R1b + R2b_, KRRb)
        self.stats_accum(pa[:], pab, 0, 1)
        sc.add("dve", I("tensor_scalar", out=SSKR, in0=pst[:], scalar1=0.5 / 192.0, scalar2=None, op0=ALU.mult), reads=pstb, writes=SSKRb)

        if DBG_STOP == 2:
            return
        wqb = self.ab_w_q_b[0].rearrange("(c p) (h x) -> p c h x", p=P, x=QKH)
        iN = self.rr("wa", 4)
        WQN = self.waflat(iN).rearrange("p (c h x) -> p c h x", c=4, h=8)
        for c in range(4):
            self.dma("pool", WQN[:, c], wqb[:, c, :, 0:128], [], [self.WA_buf[iN]], self.WA_buf[iN])
        iR = self.rr("wa", 4)
        WQR = self.waflat(iR).rearrange("p (c v h x) -> p c v h x", c=4, v=2, h=8)
        for c in range(4):
            self.dma("pool", WQR[:, c, 0, :, :], wqb[:, c, :, 128:192], [], [self.WA_buf[iR]], self.WA_buf[iR])
            self.dma("pool", WQR[:, c, 1, :, 0:32], wqb[:, c, :, 160:192], [], [self.WA_buf[iR]], self.WA_buf[iR])
            self.dma("pool", WQR[:, c, 1, :, 32:64], wqb[:, c, :, 128:160], [], [self.WA_buf[iR]], self.WA_buf[iR])
        WQRf = self.waflat(iR).rearrange("p (c v x) -> p c v x", c=4, v=2)

        def qag_rhs(k, h):
            return QAG[:, k, h * HT:(h + 1) * HT]

        def kvg_rhs(k, h):
            return KVG[:, k, h * HT:(h + 1) * HT]
        for j in range(4):
            pa, pab = self.ps_next()
            self.mm_group(pa, pab, lambda k, j=j: WQRf[:, k, 0, j * P:(j + 1) * P], qag_rhs, 4, [self.WA_buf[iR]] + QAGb)
            pb, pbb = self.ps_next()
            self.mm_group(pb, pbb, lambda k, j=j: WQRf[:, k, 1, j * P:(j + 1) * P], qag_rhs, 4, [self.WA_buf[iR]] + QAGb)
            self.dve_tt(R1[:], pa[:], CGq, ALU.mult, pab + CGqb, R1b)
            self.dve_tt(R2_[:], pb[:], SGq, ALU.mult, pbb + SGqb, R2b_)
            self.pool_tt(QRR, R1[:], R2_[:], ALU.add, R1b + R2b_, QRRb)
            sc.add("act", I("activation", out=SQR, in_=pa[:], func=AF.Square), reads=pab, writes=SQRb)
            oi = self.rr("l0ro", 2)
            QRo, QRob = A(36 + oi, 1, BF16)
            for hh in range(2):
                h = 2 * j + hh
                pc, pcb = self.ps_next()
                self.mm_group(pc, pcb, lambda k, h=h: WQN[:, k, h, :], qag_rhs, 4, [self.WA_buf[iN]] + QAGb)
                sc.add("act", I("activation", out=self.SQ[:], in_=pc[:], func=AF.Square), reads=pcb, writes=[self.SQ_buf])

                def fn(e, hh=hh):
                    inst = None
                    for hf in range(2):
                        e.matmul(pst[:, hf * HT:(hf + 1) * HT], self.ones_bf[:], self.SQ[:, hf * HT:(hf + 1) * HT], start=True, stop=False)
                        inst = e.matmul(pst[:, hf * HT:(hf + 1) * HT], self.ehalf[:, hh, :], SQR[:, hf * HT:(hf + 1) * HT], start=False, stop=True)
                    return inst
                sc.add("pe", fn, reads=[self.SQ_buf] + SQRb + cb, writes=pstb)
                self.head_stats_chain(pst, pstb, RQA2, RQA2b, None, RQA, RQAb)
                ai = self.rr("l0ao", 2)
                QAo, QAob = A(34 + ai, 1, BF16)
                sc.add("dve", I("scalar_tensor_tensor", out=QAo, in0=pc[:], scalar=self.gvec[:, 0:1], in1=RS[:], op0=ALU.mult, op1=ALU.mult),
                       reads=pcb + RSb + cb, writes=QAob)
                self.dma("sp", self.QAs[s, :, h, tsl], QAo, QAob, [self.QA_b[s][h][t]], QAob[0])
                r0, r1 = hh * 64, hh * 64 + 64
                self.pool_tt(QRo[r0:r1, :], QRR[r0:r1, :], RS[r0:r1, :], ALU.mult, QRRb + RSb, QRob)
            self.dma("sp", self.QRs[s, :, j, tsl], QRo, QRob, [self.QR_b[s][j][t]], QRob[0])

        if DBG_STOP == 3:
            return
        wkv = self.ab_w_kv_b[0].rearrange("(c p) (h x) -> p c h x", p=P, x=256)
        iK = self.rr("wa", 4)
        WKN = self.waflat(iK).rearrange("p (c h x) -> p c h x", c=4, h=8)
        for c in range(4):
            self.dma("pool", WKN[:, c], wkv[:, c, :, 0:128], [], [self.WA_buf[iK]], self.WA_buf[iK])
        iV = self.rr("wa", 4)
        WV = self.waflat(iV).rearrange("p (c x) -> p c x", c=4)
        WV4 = self.waflat(iV).rearrange("p (c h x) -> p c h x", c=4, h=8)
        for c in range(4):
            self.dma("pool", WV4[:, c], wkv[:, c, :, 128:256], [], [self.WA_buf[iV]], self.WA_buf[iV])
        for j in range(4):
            oi = self.rr("l0ro", 2)
            KRo, KRob = A(36 + oi, 1, BF16)
            for hh in range(2):
                h = 2 * j + hh
                pc, pcb = self.ps_next()
                self.mm_group(pc, pcb, lambda k, h=h: WKN[:, k, h, :], kvg_rhs, 4, [self.WA_buf[iK]] + KVGb)
                self.stats_accum(pc[:], pcb, 0, 1)
                self.head_stats_chain(pst, pstb, RKV2, RKV2b, (SSKR, SSKRb), RKV, RKVb)
                ai = self.rr("l0ao", 2)
                KAo, KAob = A(34 + ai, 1, BF16)
                sc.add("dve", I("scalar_tensor_tensor", out=KAo, in0=pc[:], scalar=self.gvec[:, 3:4], in1=RS[:], op0=ALU.mult, op1=ALU.mult),
                       reads=pcb + RSb + cb, writes=KAob)
                self.dma("sp", self.KAs[s, :, h, tsl], KAo, KAob, [self.KA_b[s][h][t]], KAob[0])
                r0, r1 = hh * 64, hh * 64 + 64
                self.pool_tt(KRo[r0:r1, :], KRR[r0:r1, :], SL[r0:r1, :], ALU.mult, KRRb + SLb, KRob)
            self.dma("sp", self.KRs[s, :, j, tsl], KRo, KRob, [self.KR_b[s][j][t]], KRob[0])

        if DBG_STOP == 4:
            return
        ps, psb = self.ps_next()

        def fnr(e):
            inst = None
            for b in range(8):
                inst = e.matmul(ps[:, 2 * b:2 * b + 2], RKV[0:1, b * P:(b + 1) * P], self.ones_f[0:1, 0:2], start=True, stop=True)
            return inst
        sc.add("pe", fnr, reads=RKVb + cb, writes=psb)
        sc.add("dve", I("tensor_copy", out=self.RKVT[:], in_=ps[:, 0:16]), reads=psb, writes=[self.RKVT_buf])
        for b in range(8):
            ps, psb = self.ps_next()

            def fnv(e, ps=ps, b=b):
                inst = None
                for k in range(4):
                    for i2 in range(2):
                        inst = e.matmul(ps[:, i2 * HT:(i2 + 1) * HT], KVG[:, k, b * P:(b + 1) * P], WV[:, k, i2 * HT:(i2 + 1) * HT],
                                        start=(k == 0), stop=(k == 3))
                return inst
            sc.add("pe", fnv, reads=KVGb + [self.WA_buf[iV]], writes=psb)
            vi = self.rr("l0vo", 2)
            Vo, Vob = A(38 + vi, 1, BF16)
            sc.add("act", I("activation", out=Vo, in_=ps[:], func=AF.Copy, scale=self.RKVT[:, 2 * b:2 * b + 1]),
                   reads=psb + [self.RKVT_buf], writes=Vob)
            self.dma("sp", self.Vs[s, t * T + b * P:t * T + (b + 1) * P, :], Vo, Vob, [self.V_b[s][0][t]], Vob[0])

        if DBG_STOP == 5:
            return
        steps = []
        for c2 in range(4):
            def load_b(c2=c2):
                return [self.load_wa(w2d, 1088 + c2 * 256)]

            def comp_b(hd, c2=c2):
                (w, wb), = hd
                for sub in range(2):
                    cc = c2 * 2 + sub
                    ps, psb = self.ps_next()
                    self.mm_group(ps, psb, lambda k, w=w, sub=sub: w[:, k, sub * P:(sub + 1) * P], self.xn_rhs, DC, [wb] + self.XN_buf)
                    X2, X2b = self.XCH[2], [self.XCH_buf[2]]
                    sc.add("act", I("activation", out=X2[:], in_=ps[:], func=AF.Copy), reads=psb, writes=X2b)
                    self.dma("sp", self.GBs[s, :, cc, tsl], X2[:], X2b, [self.GB_b[s][cc][t]], X2b[0])

            def load_cu(c2=c2):
                return [self.load_wa(w2d, 2112 + c2 * 256), self.load_wa(w2d, 3136 + c2 * 256)]

            def comp_cu(hd, c2=c2):
                (wc, wcb), (wu, wub) = hd
                for sub in range(2):
                    cc = c2 * 2 + sub
                    ps, psb = self.ps_next()
                    self.mm_group(ps, psb, lambda k, sub=sub: wc[:, k, sub * P:(sub + 1) * P], self.xn_rhs, DC, [wcb] + self.XN_buf)
                    sc.add("act", I("activation", out=SL[:], in_=ps[:], func=AF.Copy), reads=psb, writes=SLb)
                    ps, psb = self.ps_next()
                    self.mm_group(ps, psb, lambda k, sub=sub: wu[:, k, sub * P:(sub + 1) * P], self.xn_rhs, DC, [wub] + self.XN_buf)
                    X3, X3b = self.XCH[3], [self.XCH_buf[3]]
                    self.dve_tt(X3[:], ps[:], SL[:], ALU.mult, psb + SLb, X3b)
                    self.dma("sp", self.CUs[s, :, cc, tsl], X3[:], X3b, [self.CU_b[s][cc][t]], X3b[0])
            steps.append((load_b, comp_b))
            steps.append((load_cu, comp_cu))
        self.pipeline(steps)

    def ps_bank(self, b):
        return self.PS[b // 2][:, (b % 2) * HT:(b % 2 + 1) * HT], [self.PSB[b]]

    def l0_att(self, s):
        sc = self.sc
        A = self.arena
        S, NT = self.S, self.NT
        NQB, NKC = S // HT, S // P
        cb = [self.const_buf]
        scale = float(QKH) ** -0.5
        nch = S // 1024
        self.fence(40, 4)
        for h in range(8):
            slot = self.rr("attslot", 2)
            base = slot * 20
            KA, KAb = A(base, nch, BF16)
            KR, KRb = A(base + 4, nch, BF16)
            V, Vb = A(base + 8, nch, BF16)
            QA, QAb = A(base + 12, nch, BF16)
            QR, QRb = A(base + 16, nch, BF16)
            V3 = V.rearrange("p (c d) -> p c d", d=P)
            j = h // 2
            allt = range(NT)
            self.dma("sp", KA, self.KAs[s, :, h, :], [self.KA_b[s][h][t] for t in allt], KAb, KAb[0])
            self.dma("sp", KR, self.KRs[s, :, j, :], [self.KR_b[s][j][t] for t in allt], KRb, KRb[0])
            self.dma("sp", QA, self.QAs[s, :, h, :], [self.QA_b[s][h][t] for t in allt], QAb, QAb[0])
            self.dma("sp", QR, self.QRs[s, :, j, :], [self.QR_b[s][j][t] for t in allt], QRb, QRb[0])
            vsrc = self.Vs[s].rearrange("(c p) (h d) -> p c h d", p=P, d=P)
            for c4 in range(0, NKC, 8):
                self.dma("sp", V3[:, c4:c4 + 8, :], vsrc[:, c4:c4 + 8, h, :], [self.V_b[s][0][t] for t in allt], Vb, Vb[0])
            r0, r1 = (h % 2) * 64, (h % 2) * 64 + 64
            for qb in range(NQB):
                qsl = slice(qb * HT, (qb + 1) * HT)
                oi = self.rr("oacc", 2)
                O, Ob = self.ps_bank(2 * oi)
                DEN, DENb = self.ps_bank(2 * oi + 1)

                def emit_S(kc):
                    bank = 4 + self.rr("sbank", 4)
                    Sb_, Sbb = self.ps_bank(bank)
                    ksl = slice(kc * P, (kc + 1) * P)

                    def fn(e):
                        e.matmul(Sb_, KA[:, ksl], QA[:, qsl], start=True, stop=False)
                        return e.matmul(Sb_, KR[r0:r1, ksl], QR[r0:r1, qsl], start=False, stop=True)
                    sc.add("pe", fn, reads=KAb + KRb + QAb + QRb, writes=Sbb)
                    pi = self.rr("pt", 4)
                    PT, PTb = A(40, 2, BF16)
                    PTi = PT[:, pi * HT:(pi + 1) * HT]
                    PTib = [self.PT_bufs[pi]]
                    sc.add("act", I("activation", out=PTi, in_=Sb_, func=AF.Exp, scale=scale), reads=Sbb + PTb, writes=PTib)
                    return PTi, PTib

                def emit_PV(kc, pt):
                    PTi, PTib = pt

                    def fn(e):
                        e.matmul(O, V3[:, kc, :], PTi, start=(kc == 0), stop=(kc == NKC - 1))
                        return e.matmul(DEN, self.ones_bf[:], PTi, start=(kc == 0), stop=(kc == NKC - 1))
                    sc.add("pe", fn, reads=Vb + PTib + cb, writes=Ob + DENb)
                pts = {}
                for kc in range(min(2, NKC)):
                    pts[kc] = emit_S(kc)
                for kc in range(NKC):
                    if kc + 2 < NKC:
                        pts[kc + 2] = emit_S(kc + 2)
                    emit_PV(kc, pts.pop(kc))
                RD, RDb = A(42, 1, F32)
                sc.add("dve", I("reciprocal", out=RD, in_=DEN), reads=DENb, writes=RDb)
                ai = self.rr("atto", 2)
                AO, AOb_ = A(43, 1, BF16)
                AOi = AO[:, ai * HT:(ai + 1) * HT]
                AOib = [self.AO_bufs[ai]]
                self.dve_tt(AOi, O, RD, ALU.mult, Ob + RDb + AOb_, AOib)
                self.dma("sp", self.ATs[s, :, h, qsl], AOi, AOib, [self.AT_b[s][h][qb * HT // T]], AOib[0])

    def out_proj(self, s, t, w2d):
        steps = []
        for d2 in range(8):
            def load(d2=d2):
                return self.load_wa(w2d, d2 * 256)

            def comp(hd, d2=d2):
                w, wb = hd
                for sub in range(2):
                    dc = d2 * 2 + sub
                    ps, psb = self.ps_next()
                    self.mm_group(ps, psb, lambda k, w=w, sub=sub: w[:, k, sub * P:(sub + 1) * P], self.xn_rhs, DC, [wb] + self.XN_buf)
                    self.resid_step(s, t, dc, ps, psb, 1.0)
            steps.append((load, comp))
        self.pipeline(steps)

    def l0_post(self, s, t):
        sc = self.sc
        A = self.arena
        S, NT = self.S, self.NT
        cb = [self.const_buf]
        tsl = slice(t * T, (t + 1) * T)
        for h in range(8):
            self.dma("sp", self.XN[:, h, :], self.ATs[s, :, h, tsl], [self.AT_b[s][h][t]], [self.XN_buf[h]], self.XN_buf[h])
        SL, SLb = self.SL, [self.SL_buf]
        for cc in range(8):
            ci = self.rr("cu", 2)
            CUh, CUhb = A(ci * 3, 3, F32)
            GBt, GBtb = A(6 + 2 * ci, 2, F32)
            lo, hi = t * T - 1, t * T + T + 1
            a, b = 0, T + 2
            rd = [self.CU_b[s][cc][t]]
            if t == 0:
                sc.add("pool", I("memset", CUh[:, 0:1], 0.0), writes=CUhb)
                lo, a = 0, 1
            else:
                rd.append(self.CU_b[s][cc][t - 1])
            if t == NT - 1:
                sc.add("pool", I("memset", CUh[:, T + 1:T + 2], 0.0), writes=CUhb)
                hi, b = S, T + 1
            else:
                rd.append(self.CU_b[s][cc][t + 1])
            self.dma("sp", CUh[:, a:b], self.CUs[s, :, cc, lo:hi], rd, CUhb, CUhb[0])
            self.dma("sp", GBt, self.GBs[s, :, cc, tsl], [self.GB_b[s][cc][t]], GBtb, GBtb[0])
            w = self.convw
            sc.add("dve", I("tensor_scalar", out=SL[:], in0=CUh[:, 0:T], scalar1=w[:, cc, 0:1], scalar2=None, op0=ALU.mult),
                   reads=CUhb + cb, writes=SLb)
            for kk in (1, 2):
                sc.add("dve", I("scalar_tensor_tensor", out=SL[:], in0=CUh[:, kk:kk + T], scalar=w[:, cc, kk:kk + 1], in1=SL[:],
                                                                                    op0=ALU.mult, op1=ALU.add),
                       reads=CUhb + SLb + cb, writes=SLb)
            self.dve_tt(self.XN[:, 8 + cc, :], SL[:], GBt, ALU.mult, SLb + GBtb, [self.XN_buf[8 + cc]])
        self.out_proj(s, t, self.ab_w_out[0])

    def l1_pre(self, s, t):
        sc = self.sc
        A = self.arena
        cb = [self.const_buf]
        tsl = slice(t * T, (t + 1) * T)
        self.normalize(s, t, 5)
        (CGq, CGqb), (SGq, SGqb), (CGk, CGkb), (SGk, SGkb) = self.rope_tables(t, self.rope128, 0, (6, 7, 8, 9))
        w2d = self.c_w_in[0]
        src = w2d.rearrange("(c p) f -> p c f", p=P)
        pst, pstb = self.PST
        SL, SLb, RS, RSb = self.SL, [self.SL_buf], self.RS, [self.RS_buf]
        R1, R1b, R2_, R2b_, R3, R3b = self.XCH[0], [self.XCH_buf[0]], self.XCH[1], [self.XCH_buf[1]], self.XCH[2], [self.XCH_buf[2]]
        steps = []
        for hp in range(10):
            def load(hp=hp):
                c0 = hp * 256
                wn = self.load_wa(w2d, c0)
                i = self.rr("wa", 4)
                wv = self.WA[i]
                for (d0, s0) in ((0, 64), (64, 0), (128, 192), (192, 128)):
                    self.dma("pool", wv[:, :, d0:d0 + 64], src[:, :, c0 + s0:c0 + s0 + 64], [], [self.WA_buf[i]], self.WA_buf[i])
                return wn, (wv, self.WA_buf[i])

            def comp(hd, hp=hp):
                (wn, wnb), (ws, wsb) = hd
                isq = hp < 8
                CG, CGb, SG, SGb = (CGq, CGqb, SGq, SGqb) if isq else (CGk, CGkb, SGk, SGkb)
                for sub in range(2):
                    hidx = hp * 2 + sub if isq else (hp - 8) * 2 + sub
                    pa, pab = self.ps_next()
                    self.mm_group(pa, pab, lambda k, sub=sub: wn[:, k, sub * P:(sub + 1) * P], self.xn_rhs, DC, [wnb] + self.XN_buf)
                    pb, pbb = self.ps_next()
                    self.mm_group(pb, pbb, lambda k, sub=sub: ws[:, k, sub * P:(sub + 1) * P], self.xn_rhs, DC, [wsb] + self.XN_buf)
                    self.stats_accum(pa[:], pab, 0, 1)
                    sc.add("act", I("activation", out=SL[:], in_=pst[:], func=AF.Sqrt, scale=1.0 / 128.0, bias=self.epsT[:]),
                           reads=pstb + cb, writes=SLb)
                    sc.add("dve", I("reciprocal", out=RS[:], in_=SL[:]), reads=SLb, writes=RSb)
                    self.dve_tt(R1[:], pa[:], CG, ALU.mult, pab + CGb, R1b)
                    self.dve_tt(R2_[:], pb[:], SG, ALU.mult, pbb + SGb, R2b_)
                    self.pool_tt(R3[:], R1[:], R2_[:], ALU.add, R1b + R2b_, R3b)
                    oi = self.rr("l1qo", 2)
                    Qo, Qob = A(12 + oi, 1, BF16)
                    self.pool_tt(Qo, R3[:], RS[:], ALU.mult, R3b + RSb, Qob)
                    if isq:
                        self.dma("sp", self.Q1s[s, :, hidx, tsl], Qo, Qob, [self.Q1_b[s][hidx][t]], Qob[0])
                    else:
                        self.dma("sp", self.K1s[s, :, hidx, tsl], Qo, Qob, [self.K1_b[s][hidx][t]], Qob[0])
            steps.append((load, comp))
        self.pipeline(steps)
        self.fence(14, 1)
        wv0 = self.load_wa(w2d, 2560)
        wv1 = self.load_wa(w2d, 2816)
        for b in range(8):
            ps, psb = self.ps_next()

            def fnv(e, ps=ps, b=b):
                inst = None
                for k in range(DC):
                    for i2, (w, wb) in enumerate((wv0, wv1)):
                        inst = e.matmul(ps[:, i2 * HT:i2 * HT + 256], self.XN[:, k, b * P:(b + 1) * P], w[:, k, :], start=(k == 0), stop=(k == DC - 1))
                return inst
            sc.add("pe", fnv, reads=self.XN_buf + [wv0[1], wv1[1]], writes=psb)
            vi = self.rr("l1vo", 2)
            Vo, Vob = A(14, 1, BF16)
            Voi = Vo[:, vi * HT:(vi + 1) * HT]
            Voib = [self.VO_bufs[vi]]
            sc.add("act", I("activation", out=Voi.rearrange("p (a b) -> p a b", a=2), in_=ps[:].rearrange("p (a b) -> p a b", a=2)[:, :, 0:256], func=AF.Copy),
                   reads=psb + Vob, writes=Voib)
            self.dma("sp", self.V1s[s, t * T + b * P:t * T + (b + 1) * P, :], Voi, Voib, [self.V1_b[s][0][t]], Voib[0])

    def l1_att(self, s):
        sc = self.sc
        A = self.arena
        S, NT = self.S, self.NT
        NB = S // P
        cb = [self.const_buf]
        scale = 128.0 ** -0.5
        nch = S // 1024
        allt = range(NT)
        self.fence(24, 6)
        ES, ESb = A(30, 4, F32)
        ES3 = ES.rearrange("p (h q) -> p h q", h=16)
        for h in range(16):
            sc.add("act", I("activation", out=ES3[:, h, :], in_=self.zeros_f[:], func=AF.Identity, bias=self.esink[:, h:h + 1]),
                   reads=cb, writes=ESb)
        for g in range(4):
            K1t, K1b = A(0, nch, BF16)
            V1t, V1b = A(4, nch, BF16)
            Qt, Qb = A(8, 4 * nch, BF16)
            V3 = V1t.rearrange("p (c d) -> p c d", d=P)
            Q3 = Qt.rearrange("p (h q) -> p h q", h=4)
            self.dma("sp", K1t, self.K1s[s, :, g, :], [self.K1_b[s][g][t] for t in allt], K1b, K1b[0])
            vsrc = self.V1s[s].rearrange("(c p) (g d) -> p c g d", p=P, d=P)
            for c4 in range(0, NB, 8):
                self.dma("sp", V3[:, c4:c4 + 8, :], vsrc[:, c4:c4 + 8, g, :], [self.V1_b[s][0][t] for t in allt], V1b, V1b[0])
            for hh in range(4):
                self.dma("sp", Q3[:, hh, :], self.Q1s[s, :, 4 * g + hh, :], [self.Q1_b[s][4 * g + hh][t] for t in allt], Qb, Qb[0])
            for qb in range(NB):
                qsl = slice(qb * P, (qb + 1) * P)
                kbs = [kb for kb in (qb - 1, qb, qb + 1) if 0 <= kb < NB]
                oi = self.rr("oacc", 2)
                O, Ob = self.ps_bank(2 * oi)
                DEN, DENb = self.ps_bank(2 * oi + 1)
                pts = []
                for kb in kbs:
                    bank = 4 + self.rr("sbank", 4)
                    Sb_, Sbb = self.ps_bank(bank)
                    sc.add("pe", I("matmul", Sb_, K1t[:, kb * P:(kb + 1) * P], Q3[:, :, qsl], start=True, stop=True),
                           reads=K1b + Qb, writes=Sbb)
                    pi = self.rr("pt1", 6)
                    PT, PTb = A(24, 3, BF16)
                    PTi = PT[:, pi * HT:(pi + 1) * HT]
                    PTib = [self.PT1_bufs[pi]]
                    sc.add("act", I("activation", out=PTi, in_=Sb_, func=AF.Exp, scale=scale), reads=Sbb + PTb, writes=PTib)
                    if kb != qb:
                        w = 0 if kb < qb else 1
                        M = self.MASK[:, w, :, :].rearrange("p r q -> p (r q)")
                        self.pool_tt(PTi, PTi, M, ALU.mult, PTib + cb + PTb, PTib)
                    pts.append((kb, PTi, PTib))
                for i, (kb, PTi, PTib) in enumerate(pts):
                    def fn(e, i=i, kb=kb, PTi=PTi):
                        e.matmul(O, V3[:, kb, :], PTi, start=(i == 0), stop=(i == len(pts) - 1))
                        return e.matmul(DEN, self.ones_bf[:], PTi, start=(i == 0), stop=(i == len(pts) - 1))
                    sc.add("pe", fn, reads=V1b + PTib + cb, writes=Ob + DENb)
                TD, TDb = A(27, 1, F32)
                RD, RDb = A(28, 1, F32)
                self.dve_tt(TD, DEN, ES3[:, 4 * g:4 * g + 4, :].rearrange("p h q -> p (h q)"), ALU.add, DENb + ESb, TDb)
                sc.add("dve", I("reciprocal", out=RD, in_=TD), reads=TDb, writes=RDb)
                ai = self.rr("att1o", 2)
                AO, AOb_ = A(29, 1, BF16)
                AOi = AO[:, ai * HT:(ai + 1) * HT]
                AOib = [self.AO1_bufs[ai]]
                self.dve_tt(AOi, O, RD, ALU.mult, Ob + RDb + AOb_, AOib)
                self.dma("sp", self.A1s[s, :, 4 * g:4 * g + 4, qsl], AOi.rearrange("p (h q) -> p h q", h=4), AOib,
                         [self.A1_b[s][g][qb * P // T]], AOib[0])

    def l1_post(self, s, t):
        tsl = slice(t * T, (t + 1) * T)
        for h in range(16):
            self.dma("sp", self.XN[:, h, :], self.A1s[s, :, h, tsl], [self.A1_b[s][h // 4][t]], [self.XN_buf[h]], self.XN_buf[h])
        self.out_proj(s, t, self.c_w_out[0])

    def build(self):
        self.declare()
        self.load_consts()
        self.load_consts2()
        st = self.stages
        NTr = range(self.NT)
        for s in range(self.nseq):
            if st == "ffn1":
                for t in NTr:
                    self.tr_in(s, t)
                    self.ffn(s, t, 0, 0, do_stats=False)
                    self.tr_out(s, t)
                continue
            if st == "none":
                for t in NTr:
                    self.tr_in(s, t)
                    self.tr_out(s, t)
                continue
            lvl = 9 if st == "all" else float(st)
            for t in NTr:
                self.tr_in(s, t)
                self.ffn(s, t, 0, 0)
                self.l0_pre(s, t)
            if lvl >= 0.5:
                self.l0_att(s)
            for t in NTr:
                if lvl >= 1:
                    self.l0_post(s, t)
                if lvl >= 2:
                    self.ffn(s, t, 0, 1)
                    self.ffn(s, t, 1, 0)
                    self.l1_pre(s, t)
            if lvl >= 2:
                self.l1_att(s)
            for t in NTr:
                if lvl >= 2:
                    self.l1_post(s, t)
                    self.ffn(s, t, 1, 1, do_stats=False)
                self.tr_out(s, t)
        self.sc.emit(self.nc, self.stack)
        self.stack.close()
        return self.nc


def host_consts(S):
    pos = np.arange(S, dtype=np.float32)
    out = {}
    for dim, name, rep in ((64, "rope64", 2), (128, "rope128", 1)):
        half = dim // 2
        inv = (1.0 / (10000.0 ** (np.arange(0, dim, 2, dtype=np.float32) / np.float32(dim)))).astype(np.float32)
        ang = (pos[:, None] * inv[None, :]).astype(np.float32)
        cos = np.cos(ang).astype(np.float32).T
        sin = np.sin(ang).astype(np.float32).T
        c = np.concatenate([cos, cos] * rep, axis=0)
        sg = np.concatenate([-sin, sin] * rep, axis=0)
        out[name] = np.ascontiguousarray(np.stack([c, sg]).astype(np.float32))
    j = np.arange(128)[:, None]
    i = np.arange(128)[None, :]
    out["masks"] = np.stack([(j >= i), (j <= i)]).astype(np.float32)
    out["ident_in"] = np.eye(128, dtype=np.float32)
    return out


WEIGHT_NAMES = ["ffn_norm", "ffn_w_gate", "ffn_w_up", "ffn_w_down", "mix_norm", "ab_w_in", "ab_q_a_norm", "ab_w_q_b",
                "ab_kv_a_norm", "ab_w_kv_b", "ab_q_norm", "ab_k_norm", "ab_conv_w", "ab_w_out", "c_w_in", "c_q_norm",
                "c_k_norm", "c_sink", "c_w_out"]


def kernel(x_prompt, x_sample, **weights):
    seqs = [np.asarray(x_prompt[i]) for i in range(x_prompt.shape[0])] + [np.asarray(x_sample[i]) for i in range(x_sample.shape[0])]
    nseq_tot = len(seqs)
    ncores = 8
    S = seqs[0].shape[0]
    assign = []
    for c in range(ncores):
        a = [c, c + 8 if c + 8 < nseq_tot else c]
        assign.append(a)
    b = Builder(2, S, stages="all")
    nc = b.build()
    consts = host_consts(S)
    wts = {k: np.ascontiguousarray(np.asarray(weights[k], dtype=np.float32)) for k in WEIGHT_NAMES}
    in_maps = []
    for c in range(ncores):
        m = {"x": np.ascontiguousarray(np.stack([seqs[i] for i in assign[c]]).astype(np.float32))}
        m.update(wts)
        m.update(consts)
        in_maps.append(m)
    res = run_bass_kernel_spmd(nc, in_maps, core_ids=list(range(ncores)))
    outs = [None] * nseq_tot
    for c in range(ncores):
        y = res.results[c]["y"]
        for k, i in enumerate(assign[c]):
            if outs[i] is None:
                outs[i] = y[k]
    npr = x_prompt.shape[0]
    return (np.stack(outs[:npr]).astype(np.float32), np.stack(outs[npr:]).astype(np.float32))
```

```python
import math
from contextlib import ExitStack

import numpy as np
import concourse.bass as bass
import concourse.mybir as mybir
from concourse.bass_utils import run_bass_kernel_spmd

F32 = mybir.dt.float32
BF16 = mybir.dt.bfloat16
AF = mybir.ActivationFunctionType
ALU = mybir.AluOpType

import os
DBG_STOP = int(os.environ.get('DBG_STOP', '0'))
DBG_OUT = int(os.environ.get('DBG_OUT', '0'))
P = 128
T = 1024
HT = 512
D = 2048
DC = 16
DFF = 5632
FC = 44
EPS = 1e-6
S_FULL = 4096
NH = 8
QL = 512
KVL = 512
NOPE = 128
ROPE = 64
QKH = 192
ABIN = 4160
CH = 16
CKV = 4
CIN = 3072


def I(name, *args, **kw):
    return [(name, args, kw)]


class Rec:
    def __init__(self):
        self.ops = []

    def __getattr__(self, name):
        def f(*args, **kw):
            self.ops.append((name, args, kw))
            return None
        return f


class Buf:
    __slots__ = ("name", "lastw", "readers", "sem", "dma_count", "excl")

    def __init__(self, name, excl=False):
        self.name = name
        self.excl = excl
        self.lastw = None
        self.readers = {}
        self.sem = None
        self.dma_count = 0


class Op:
    __slots__ = ("eng", "fn", "deps", "count", "dma_buf", "semval", "need_inc")


class Sched:
    ENGS = ("pe", "act", "dve", "pool", "sp")

    def __init__(self):
        self.ops = {e: [] for e in self.ENGS}
        self.dma_bufs = []

    def add(self, eng, fn, reads=(), writes=(), dma_buf=None):
        if callable(fn):
            rec = Rec()
            fn(rec)
            fn = rec.ops
        op = Op()
        op.eng = eng
        op.fn = fn
        op.count = 0
        op.need_inc = False
        op.dma_buf = dma_buf
        op.semval = 0
        deps = set()
        xr = [b for b in reads if b.excl]
        if xr:
            reads = [b for b in reads if not b.excl]
            writes = list(writes) + [b for b in xr if b not in writes]
        for b in reads:
            if b.lastw is not None:
                deps.add(b.lastw)
        for b in writes:
            if b.lastw is not None:
                deps.add(b.lastw)
            deps.update(b.readers.values())
        deps.discard(op)
        op.deps = deps
        if dma_buf is not None:
            if dma_buf.dma_count == 0:
                self.dma_bufs.append(dma_buf)
            dma_buf.dma_count += 1
            op.semval = 16 * dma_buf.dma_count
            key = ("dma", id(dma_buf))
        else:
            key = eng
        for b in reads:
            b.readers[key] = op
        for b in writes:
            b.lastw = op
            b.readers = {}
        self.ops[eng].append(op)
        return op

    def emit(self, nc, stack):
        engsem = {}
        for e in ("pe", "act", "dve", "pool"):
            engsem[e] = stack.enter_context(nc.semaphore("s_" + e))
        for i, b in enumerate(self.dma_bufs):
            b.sem = stack.enter_context(nc.semaphore("d%d" % i))
        for e in self.ENGS:
            for op in self.ops[e]:
                for d in op.deps:
                    if d.dma_buf is None:
                        d.need_inc = True
        for e in ("pe", "act", "dve", "pool"):
            c = 0
            for op in self.ops[e]:
                if op.dma_buf is None and op.need_inc:
                    c += 1
                    op.count = c
        block = stack.enter_context(nc.Block())
        engobj = {"pe": "tensor", "act": "scalar", "dve": "vector", "pool": "gpsimd", "sp": "sync"}

        def make(e):
            ops = self.ops[e]

            def body(eng):
                waited = {}
                for op in ops:
                    need = {}
                    for d in op.deps:
                        if d.dma_buf is not None:
                            s, v = d.dma_buf.sem, d.semval
                        else:
                            if d.eng == "pe" and e == "pe":
                                continue
                            s, v = engsem[d.eng], d.count
                        k = id(s)
                        if v > need.get(k, (None, 0))[1]:
                            need[k] = (s, v)
                    for k, (s, v) in need.items():
                        if v > waited.get(k, 0):
                            eng.wait_ge(s, v)
                            waited[k] = v
                    inst = None
                    for (name, args, kw) in op.fn:
                        inst = getattr(eng, name)(*args, **kw)
                    if op.dma_buf is not None:
                        inst.then_inc(op.dma_buf.sem, 16)
                    elif op.need_inc:
                        inst.then_inc(engsem[e], 1)
                if e in ("sp", "pool"):
                    last = {}
                    for op in ops:
                        if op.dma_buf is not None:
                            last[id(op.dma_buf)] = (op.dma_buf.sem, max(op.semval, last.get(id(op.dma_buf), (None, 0))[1]))
                    for k, (s, v) in last.items():
                        if v > waited.get(k, 0):
                            eng.wait_ge(s, v)
            return body

        for e in self.ENGS:
            getattr(block, engobj[e])(make(e))


class Builder:
    def __init__(self, nseq, S, stages="all"):
        self.nseq = nseq
        self.S = S
        self.NT = S // T
        self.stages = stages
        self.sc = Sched()
        self.nc = bass.Bass("TRN2", target_bir_lowering=False)
        self.stack = ExitStack()
        self._rr = {}

    def dram_in(self, name, shape, dt=F32):
        return self.nc.dram_tensor(name, list(shape), dt, kind="ExternalInput").ap()

    def sb(self, name, shape, dt):
        return self.stack.enter_context(self.nc.sbuf_tensor(name, list(shape), dt))

    def rr(self, key, n):
        i = self._rr.get(key, 0)
        self._rr[key] = i + 1
        return i % n

    def dma(self, q, out, in_, reads, writes, dma_buf, slow=False):
        self.sc.add(q, I("dma_start", out=out, in_=in_, allow_slow_non_contiguous=slow),
                    reads=reads, writes=writes, dma_buf=dma_buf)

    def declare(self):
        nc, S, nseq = self.nc, self.S, self.nseq
        self.x_in = self.dram_in("x", [nseq, S, D])
        self.ffn_norm = self.dram_in("ffn_norm", [2, 2, D])
        self.w_gate = self.dram_in("ffn_w_gate", [2, 2, D, DFF])
        self.w_up = self.dram_in("ffn_w_up", [2, 2, D, DFF])
        self.w_down = self.dram_in("ffn_w_down", [2, 2, DFF, D])
        self.mix_norm = self.dram_in("mix_norm", [2, D])
        self.ab_w_in = self.dram_in("ab_w_in", [1, D, ABIN])
        self.ab_q_a_norm = self.dram_in("ab_q_a_norm", [1, QL])
        self.ab_w_q_b = self.dram_in("ab_w_q_b", [1, QL, NH * QKH])
        self.ab_kv_a_norm = self.dram_in("ab_kv_a_norm", [1, KVL])
        self.ab_w_kv_b = self.dram_in("ab_w_kv_b", [1, KVL, NH * 256])
        self.ab_q_norm = self.dram_in("ab_q_norm", [1, QKH])
        self.ab_k_norm = self.dram_in("ab_k_norm", [1, QKH])
        self.ab_conv_w = self.dram_in("ab_conv_w", [1, 3, 1024])
        self.ab_w_out = self.dram_in("ab_w_out", [1, D, D])
        self.c_w_in = self.dram_in("c_w_in", [1, D, CIN])
        self.c_q_norm = self.dram_in("c_q_norm", [1, 128])
        self.c_k_norm = self.dram_in("c_k_norm", [1, 128])
        self.c_sink = self.dram_in("c_sink", [1, CH])
        self.c_w_out = self.dram_in("c_w_out", [1, D, D])
        self.rope64 = self.dram_in("rope64", [2, 128, S])
        self.rope128 = self.dram_in("rope128", [2, 128, S])
        self.masks = self.dram_in("masks", [2, 128, 128])
        self.y_out = nc.dram_tensor("y", [nseq, S, D], F32, kind="ExternalOutput").ap()
        self.XT = nc.dram_tensor("XT", [nseq, P, DC, S], F32, kind="Internal").ap()
        self.XT_buf = [[[Buf("XT%d_%d_%d" % (s, t, c)) for c in range(DC)] for t in range(self.NT)] for s in range(nseq)]
        self.XN = self.sb("XN", [P, DC, T], BF16)
        self.XN_buf = [Buf("XN%d" % c) for c in range(DC)]
        self.HID = self.sb("HID", [P, FC, T], BF16)
        self.HID_buf = [Buf("HID%d" % c) for c in range(FC)]
        self.WA = [self.sb("WA%d" % i, [P, DC, 256], BF16) for i in range(4)]
        self.WA_buf = [Buf("WA%d" % i) for i in range(4)]
        self.WB = [self.sb("WB%d" % i, [P, FC, 128], BF16) for i in range(2)]
        self.WB_buf = [Buf("WB%d" % i) for i in range(2)]
        self.XCH = [self.sb("XCH%d" % i, [P, T], F32) for i in range(4)]
        self.XCH_buf = [Buf("XCH%d" % i) for i in range(4)]
        self.SQ = self.sb("SQ", [P, T], BF16)
        self.SQ_buf = Buf("SQ")
        self.SL = self.sb("SL", [P, T], F32)
        self.SL_buf = Buf("SL")
        self.RS = self.sb("RS", [P, T], F32)
        self.RS_buf = Buf("RS")
        self.ones_bf = self.sb("ones_bf", [P, P], BF16)
        self.ident = self.sb("ident", [P, P], F32)
        self.const_buf = Buf("const")
        self.gains = self.sb("gains", [P, 6, DC], F32)
        self.epsT = self.sb("epsT", [P, 1], F32)
        self.ident_in = self.dram_in("ident_in", [P, P])
        self.PS = [self.stack.enter_context(nc.psum_tensor("PS%d" % i, [P, T], F32)) for i in range(4)]
        self.PSB = [Buf("PSB%d" % i, excl=True) for i in range(8)]
        self.PS_buf = [[self.PSB[2 * i], self.PSB[2 * i + 1]] for i in range(4)]
        self.declare2()

    def load_consts(self):
        sc = self.sc
        cb = [self.const_buf]
        sc.add("dve", I("memset", self.ones_bf[:], 1.0), writes=cb)
        sc.add("dve", I("memset", self.epsT[:], EPS), writes=cb)
        self.cdma = Buf("cdma")
        self.dma("sp", self.ident[:], self.ident_in, [], cb, self.cdma)
        for l in range(2):
            for j in range(2):
                self.dma("sp", self.gains[:, l * 2 + j, :], self.ffn_norm[l, j].rearrange("(c p) -> p c", p=P), [], cb, self.cdma, slow=True)
            self.dma("sp", self.gains[:, 4 + l, :], self.mix_norm[l].rearrange("(c p) -> p c", p=P), [], cb, self.cdma, slow=True)

    def ps_next(self):
        i = self.rr("ps", 3)
        return self.PS[i], self.PS_buf[i]

    @property
    def PST(self):
        return self.PS[3], self.PS_buf[3]

    def mm_group(self, ps, ps_buf, lhs_fn, rhs_fn, K, reads, width=T):
        nh = width // HT

        def fn(e):
            inst = None
            for k in range(K):
                for h in range(nh):
                    inst = e.matmul(ps[:, h * HT:(h + 1) * HT], lhs_fn(k), rhs_fn(k, h), start=(k == 0), stop=(k == K - 1))
            return inst
        self.sc.add("pe", fn, reads=list(reads) + [self.const_buf], writes=ps_buf)
        self.flush_pe()

    def flush_pe(self):
        pend, self._pend = getattr(self, "_pend", []), []
        for (fn, reads, writes) in pend:
            self.sc.add("pe", fn, reads=reads, writes=writes)

    def stats_accum(self, src_ap, src_buf, c, nchunks=DC, defer=False):
        sc = self.sc
        SQ, SQb = self.SQ, self.SQ_buf
        sc.add("act", I("activation", out=SQ[:], in_=src_ap, func=AF.Square), reads=src_buf, writes=[SQb])
        pst, pstb = self.PST

        def fn(e):
            inst = None
            for h in range(2):
                inst = e.matmul(pst[:, h * HT:(h + 1) * HT], self.ones_bf[:], SQ[:, h * HT:(h + 1) * HT],
                                start=(c == 0), stop=(c == nchunks - 1))
            return inst
        if defer:
            rec = Rec()
            fn(rec)
            self._pend = getattr(self, "_pend", []) + [(rec.ops, [SQb, self.const_buf], pstb)]
        else:
            sc.add("pe", fn, reads=[SQb, self.const_buf], writes=pstb)

    def stats_final(self, n, out=None, outb=None):
        self.flush_pe()
        sc = self.sc
        pst, pstb = self.PST
        RS = self.RS[:] if out is None else out
        RSb = [self.RS_buf] if outb is None else outb
        sc.add("act", I("activation", out=RS, in_=pst[:], func=AF.Sqrt, scale=1.0 / n, bias=self.epsT[:]),
               reads=pstb + [self.const_buf], writes=RSb)
        sc.add("dve", I("reciprocal", out=RS, in_=RS), reads=RSb, writes=RSb)

    def xchunk_out(self, s, t, c, xi, do_stats=True):
        XT_ap = self.XT[s, :, c, t * T:(t + 1) * T]
        self.dma("sp", XT_ap, self.XCH[xi][:], [self.XCH_buf[xi]], [self.XT_buf[s][t][c]], self.XCH_buf[xi])
        if do_stats:
            self.stats_accum(self.XCH[xi][:], [self.XCH_buf[xi]], c, defer=True)
            if c == DC - 1:
                self.stats_final(float(D))

    def resid_step(self, s, t, c, ps, ps_buf, scale, do_stats=True):
        xi = self.rr("xch", 4)
        XT_ap = self.XT[s, :, c, t * T:(t + 1) * T]
        X = self.XCH[xi]
        self.dma("sp", X[:], XT_ap, [self.XT_buf[s][t][c]], [self.XCH_buf[xi]], self.XCH_buf[xi])
        self.sc.add("dve", I("scalar_tensor_tensor", out=X[:], in0=ps[:], scalar=scale, in1=X[:], op0=ALU.mult, op1=ALU.add),
                    reads=ps_buf + [self.XCH_buf[xi]], writes=[self.XCH_buf[xi]])
        self.xchunk_out(s, t, c, xi, do_stats)

    def normalize(self, s, t, gidx):
        for c in range(DC):
            xi = self.rr("xch", 4)
            X = self.XCH[xi]
            XT_ap = self.XT[s, :, c, t * T:(t + 1) * T]
            self.dma("sp", X[:], XT_ap, [self.XT_buf[s][t][c]], [self.XCH_buf[xi]], self.XCH_buf[xi])
            g = self.gains[:, gidx, c:c + 1]
            self.sc.add("dve", I("scalar_tensor_tensor", out=self.XN[:, c, :], in0=X[:], scalar=g, in1=self.RS[:],
                                                                               op0=ALU.mult, op1=ALU.mult),
                        reads=[self.XCH_buf[xi], self.RS_buf, self.const_buf], writes=[self.XN_buf[c]])

    def load_wa(self, w2d, c0, ncols=256, K=DC):
        i = self.rr("wa", 4)
        src = w2d.rearrange("(c p) f -> p c f", p=P)[:, :, c0:c0 + ncols]
        self.dma("pool", self.WA[i][:, :K, :ncols], src, [], [self.WA_buf[i]], self.WA_buf[i])
        return self.WA[i], self.WA_buf[i]

    def load_wb(self, w2d, c0):
        i = self.rr("wb", 4)
        src = w2d.rearrange("(c p) f -> p c f", p=P)
        if i < 2:
            for a, b in ((0, 16), (16, 32), (32, FC)):
                self.dma("pool", self.WB[i][:, a:b, :], src[:, a:b, c0:c0 + 128], [], [self.WB_buf[i]], self.WB_buf[i])
            W = self.WB[i]
            return (lambda k: W[:, k, :]), [self.WB_buf[i]]
        j = (i - 2) * 2
        pieces = []
        for q in range(2):
            v = self.waflat(j + q)[:, 0:22 * 128].rearrange("p (c f) -> p c f", c=22)
            for a, b in ((0, 11), (11, 22)):
                self.dma("pool", v[:, a:b, :], src[:, q * 22 + a:q * 22 + b, c0:c0 + 128], [], [self.WA_buf[j + q]], self.WA_buf[j + q])
            pieces.append(v)
        return (lambda k: pieces[k // 22][:, k % 22, :]), [self.WA_buf[j], self.WA_buf[j + 1]]

    def tr_in(self, s, t):
        nc, sc = self.nc, self.sc
        XR = self.HID[:].rearrange("p c t -> p (c t)")[:, 0:32 * T].bitcast(F32).rearrange("p (b f) -> p b f", b=8)
        for b in range(8):
            bufs = self.HID_buf[b * 4:(b + 1) * 4]
            self.dma("sp", XR[:, b, :], self.x_in[s, t * T + b * P: t * T + (b + 1) * P, :], [], bufs, bufs[0])
        for c in range(DC):
            ps, psb = self.ps_next()

            def fn(e, ps=ps, c=c):
                inst = None
                for b in range(8):
                    inst = e.transpose(out=ps[:, b * P:(b + 1) * P], in_=XR[:, b, c * P:(c + 1) * P], identity=self.ident[:])
                return inst
            sc.add("pe", fn, reads=self.HID_buf[0:32] + [self.const_buf], writes=psb)
            self.flush_pe()
            xi = self.rr("xch", 4)
            X = self.XCH[xi]
            sc.add("dve", I("tensor_copy", out=X[:], in_=ps[:]), reads=psb, writes=[self.XCH_buf[xi]])
            self.xchunk_out(s, t, c, xi)

    def tr_out(self, s, t):
        sc = self.sc
        XF = self.HID[:].rearrange("p c t -> p (c t)")[:, 0:32 * T].bitcast(F32).rearrange("p (c f) -> p c f", c=DC)
        for c in range(DC):
            bufs = self.HID_buf[c * 2:(c + 1) * 2]
            self.dma("sp", XF[:, c, :], self.XT[s, :, c, t * T:(t + 1) * T], [self.XT_buf[s][t][c]], bufs, bufs[0])
        OUT = self.XN[:].rearrange("p c t -> p (c t)").bitcast(F32).rearrange("p (o f) -> p o f", o=4)
        for b in range(8):
            oi = self.rr("outst", 4)
            obufs = self.XN_buf[oi * 4:(oi + 1) * 4]
            for h in range(2):
                ps, psb = self.ps_next()

                def fn(e, ps=ps, h=h, b=b):
                    inst = None
                    for i in range(8):
                        c = h * 8 + i
                        inst = e.transpose(out=ps[:, i * P:(i + 1) * P], in_=XF[:, c, b * P:(b + 1) * P], identity=self.ident[:])
                    return inst
                sc.add("pe", fn, reads=self.HID_buf[0:32] + [self.const_buf], writes=psb)
                eng = "dve" if h == 0 else "act"
                if eng == "dve":
                    sc.add("dve", I("tensor_copy", out=OUT[:, oi, h * T:(h + 1) * T], in_=ps[:]),
                           reads=psb, writes=obufs[h * 2:(h + 1) * 2])
                else:
                    sc.add("act", I("activation", out=OUT[:, oi, h * T:(h + 1) * T], in_=ps[:], func=AF.Copy),
                           reads=psb, writes=obufs[h * 2:(h + 1) * 2])
            self.dma("sp", self.y_out[s, t * T + b * P: t * T + (b + 1) * P, :], OUT[:, oi, :], obufs, [], obufs[0])

    def ffn(self, s, t, l, j, do_stats=True):
        sc = self.sc
        self.normalize(s, t, l * 2 + j)
        wg2d, wu2d, wd2d = self.w_gate[l, j], self.w_up[l, j], self.w_down[l, j]
        NF2 = FC // 2
        loads = [None] * (NF2 + 1)
        loads[0] = (self.load_wa(wg2d, 0), self.load_wa(wu2d, 0))
        for f2 in range(NF2):
            if f2 + 1 < NF2:
                loads[f2 + 1] = (self.load_wa(wg2d, (f2 + 1) * 256), self.load_wa(wu2d, (f2 + 1) * 256))
            (wg, wgb), (wu, wub) = loads[f2]
            if f2 == NF2 - 3:
                nxt = [self.load_wb(wd2d, 0), self.load_wb(wd2d, P)]
                nload = 2
            for sub in range(2):
                fc = f2 * 2 + sub
                pg, pgb = self.ps_next()
                self.mm_group(pg, pgb, lambda k, wg=wg, sub=sub: wg[:, k, sub * P:(sub + 1) * P],
                              lambda k, h: self.XN[:, k, h * HT:(h + 1) * HT], DC, [wgb] + self.XN_buf)
                pu, pub = self.ps_next()
                self.mm_group(pu, pub, lambda k, wu=wu, sub=sub: wu[:, k, sub * P:(sub + 1) * P],
                              lambda k, h: self.XN[:, k, h * HT:(h + 1) * HT], DC, [wub] + self.XN_buf)
                sc.add("act", I("activation", out=self.SL[:], in_=pg[:], func=AF.Silu), reads=pgb, writes=[self.SL_buf])
                sc.add("dve", I("tensor_tensor", out=self.HID[:, fc, :], in0=pu[:], in1=self.SL[:], op=ALU.mult),
                       reads=pub + [self.SL_buf], writes=[self.HID_buf[fc]])
        for dc in range(DC):
            while len(nxt) < 3 and nload < DC:
                nxt.append(self.load_wb(wd2d, nload * P))
                nload += 1
            wd, wdb = nxt.pop(0)
            py, pyb = self.ps_next()
            self.mm_group(py, pyb, wd, lambda k, h: self.HID[:, k, h * HT:(h + 1) * HT], FC, wdb + self.HID_buf)
            self.resid_step(s, t, dc, py, pyb, 0.5, do_stats)

    def declare2(self):
        nc, S, nseq, NT = self.nc, self.S, self.nseq, self.NT

        def scratch(name, shape, dt):
            return nc.dram_tensor(name, list(shape), dt, kind="ExternalOutput" if DBG_OUT else "Internal").ap()

        def bufs(name, n1):
            return [[[Buf("%s%d_%d_%d" % (name, s, i, t)) for t in range(NT)] for i in range(n1)] for s in range(nseq)]
        self.QAs = scratch("QAs", [nseq, P, 8, S], BF16); self.QA_b = bufs("QA", 8)
        self.QRs = scratch("QRs", [nseq, P, 8, S], BF16); self.QR_b = bufs("QR", 8)
        self.KAs = scratch("KAs", [nseq, P, 8, S], BF16); self.KA_b = bufs("KA", 8)
        self.KRs = scratch("KRs", [nseq, P, 4, S], BF16); self.KR_b = bufs("KR", 4)
        self.Vs = scratch("Vs", [nseq, S, 1024], BF16); self.V_b = bufs("V", 1)
        self.CUs = scratch("CUs", [nseq, P, 8, S], F32); self.CU_b = bufs("CU", 8)
        self.GBs = scratch("GBs", [nseq, P, 8, S], F32); self.GB_b = bufs("GB", 8)
        self.ATs = scratch("ATs", [nseq, P, 8, S], BF16); self.AT_b = bufs("AT", 8)
        self.Q1s = scratch("Q1s", [nseq, P, 16, S], BF16); self.Q1_b = bufs("Q1", 16)
        self.K1s = scratch("K1s", [nseq, P, 4, S], BF16); self.K1_b = bufs("K1", 4)
        self.V1s = scratch("V1s", [nseq, S, 512], BF16); self.V1_b = bufs("V1", 1)
        self.A1s = scratch("A1s", [nseq, P, 16, S], BF16); self.A1_b = bufs("A1", 4)
        self.gqa = self.sb("gqa", [P, 4], F32)
        self.gkva = self.sb("gkva", [P, 4], F32)
        self.gvec = self.sb("gvec", [P, 10], F32)
        self.convw = self.sb("convw", [P, 8, 3], F32)
        self.esink = self.sb("esink", [P, 16], F32)
        self.sink1 = self.sb("sink1", [1, 16], F32)
        self.ones_f = self.sb("ones_f", [1, P], F32)
        self.zeros_f = self.sb("zeros_f", [P, P], F32)
        self.ehalf = self.sb("ehalf", [P, 2, P], BF16)
        self.MASK = self.sb("MASK", [P, 2, 4, P], BF16)
        self.RKVT = self.sb("RKVT", [P, 16], F32)
        self.RKVT_buf = Buf("RKVT")
        self.HIDflat = self.HID[:].rearrange("p c t -> p (c t)")
        self.PT_bufs = [Buf("PT%d" % i) for i in range(4)]
        self.AO_bufs = [Buf("AO%d" % i) for i in range(2)]
        self.VO_bufs = [Buf("VO%d" % i) for i in range(2)]
        self.PT1_bufs = [Buf("PT1_%d" % i) for i in range(6)]
        self.AO1_bufs = [Buf("AO1_%d" % i) for i in range(2)]

    def arena(self, chunk0, nchunks, dt):
        e0 = chunk0 * 1024
        ap = self.HIDflat[:, e0:e0 + nchunks * 1024]
        if dt == F32:
            ap = ap.bitcast(F32)
        return ap, self.HID_buf[chunk0:chunk0 + nchunks]

    def fence(self, chunk0, n):
        ap, b = self.arena(chunk0, n, BF16)
        self.sc.add("pool", I("memset", ap[:, 0:2], 0.0), writes=b)

    def waflat(self, i):
        return self.WA[i][:].rearrange("p c f -> p (c f)")

    def load_consts2(self):
        sc = self.sc
        cb = [self.const_buf]
        cd = self.cdma
        sc.add("dve", I("memset", self.ones_f[:], 1.0), writes=cb)
        sc.add("dve", I("memset", self.zeros_f[:], 0.0), writes=cb)
        sc.add("dve", I("memset", self.ehalf[:], 0.0), writes=cb)
        sc.add("dve", I("memset", self.ehalf[0:64, 0, :], 1.0), writes=cb)
        sc.add("dve", I("memset", self.ehalf[64:128, 1, :], 1.0), writes=cb)
        self.dma("sp", self.gqa[:], self.ab_q_a_norm[0].rearrange("(c p) -> p c", p=P), [], cb, cd, slow=True)
        self.dma("sp", self.gkva[:], self.ab_kv_a_norm[0].rearrange("(c p) -> p c", p=P), [], cb, cd, slow=True)

        def col(v, a, b):
            return v[a:b].rearrange("(p o) -> p o", o=1)
        gv = self.gvec
        for base, vec in ((0, self.ab_q_norm[0]), (3, self.ab_k_norm[0])):
            self.dma("sp", gv[:, base:base + 1], col(vec, 0, 128), [], cb, cd)
            for r in range(2):
                self.dma("sp", gv[r * 64:(r + 1) * 64, base + 1:base + 2], col(vec, 128, 192), [], cb, cd)
                self.dma("sp", gv[r * 64:r * 64 + 32, base + 2:base + 3], col(vec, 160, 192), [], cb, cd)
                self.dma("sp", gv[r * 64 + 32:r * 64 + 64, base + 2:base + 3], col(vec, 128, 160), [], cb, cd)
        for base, vec in ((6, self.c_q_norm[0]), (8, self.c_k_norm[0])):
            self.dma("sp", gv[:, base:base + 1], col(vec, 0, 128), [], cb, cd)
            self.dma("sp", gv[0:64, base + 1:base + 2], col(vec, 64, 128), [], cb, cd)
            self.dma("sp", gv[64:128, base + 1:base + 2], col(vec, 0, 64), [], cb, cd)
        for kk in range(3):
            self.dma("sp", self.convw[:, :, kk], self.ab_conv_w[0, kk].rearrange("(c p) -> p c", p=P), [], cb, cd, slow=True)
        self.dma("sp", self.sink1[:], self.c_sink, [], cb, cd)
        mb = Buf("maskdma")
        for w in range(2):
            for r in range(4):
                self.dma("pool", self.MASK[:, w, r, :], self.masks[w], [], cb, mb)
        ps, psb = self.ps_next()
        sc.add("pe", I("matmul", ps[:, 0:16], self.ones_f[0:1, :], self.sink1[0:1, :], start=True, stop=True), reads=cb, writes=psb)
        sc.add("act", I("activation", out=self.esink[:], in_=ps[:, 0:16], func=AF.Exp), reads=psb, writes=cb)

    def pipeline(self, steps, ahead=1):
        handles = {}
        n = len(steps)
        for i in range(min(ahead, n)):
            handles[i] = steps[i][0]()
        for i in range(n):
            if i + ahead < n:
                handles[i + ahead] = steps[i + ahead][0]()
            steps[i][1](handles.pop(i))

    def xn_rhs(self, k, h):
        return self.XN[:, k, h * HT:(h + 1) * HT]

    def pool_tt(self, out, in0, in1, op, reads, writes):
        self.sc.add("pool", I("tensor_tensor", out=out, in0=in0, in1=in1, op=op), reads=reads, writes=writes)

    def dve_tt(self, out, in0, in1, op, reads, writes):
        self.sc.add("dve", I("tensor_tensor", out=out, in0=in0, in1=in1, op=op), reads=reads, writes=writes)

    def rope_tables(self, t, table, c0, gidx):
        tsl = slice(t * T, (t + 1) * T)
        TC, TCb = self.arena(c0, 2, F32)
        TS, TSb = self.arena(c0 + 2, 2, F32)
        self.dma("sp", TC, table[0][:, tsl], [], TCb, TCb[0])
        self.dma("sp", TS, table[1][:, tsl], [], TSb, TSb[0])
        out = []
        for i, (src, srcb, gi) in enumerate(((TC, TCb, gidx[0]), (TS, TSb, gidx[1]), (TC, TCb, gidx[2]), (TS, TSb, gidx[3]))):
            dst, dstb = self.arena(c0 + 4 + 2 * i, 2, F32)
            g = self.gvec[:, gi:gi + 1]
            self.sc.add("act", I("activation", out=dst, in_=src, func=AF.Copy, scale=g),
                        reads=srcb + [self.const_buf], writes=dstb)
            out.append((dst, dstb))
        return out

    def head_stats_chain(self, pst, pstb, E, Eb):
        sc = self.sc
        SL, SLb, RS, RSb = self.SL, [self.SL_buf], self.RS, [self.RS_buf]
        sc.add("dve", I("scalar_tensor_tensor", out=SL[:], in0=pst[:], scalar=1.0 / 192.0, in1=E, op0=ALU.mult, op1=ALU.add),
               reads=pstb + Eb, writes=SLb)
        sc.add("act", I("activation", out=SL[:], in_=SL[:], func=AF.Sqrt), reads=SLb, writes=SLb)
        sc.add("dve", I("reciprocal", out=RS[:], in_=SL[:]), reads=SLb, writes=RSb)

    def l0_pre(self, s, t):
        sc = self.sc
        A = self.arena
        cb = [self.const_buf]
        tsl = slice(t * T, (t + 1) * T)
        self.normalize(s, t, 4)
        if DBG_STOP == 11:
            return
        (CGq, CGqb), (SGq, SGqb), (CGk, CGkb), (SGk, SGkb) = self.rope_tables(t, self.rope64, 16, (1, 2, 4, 5))
        if DBG_STOP == 12:
            return
        w2d = self.ab_w_in[0]
        QAG, QAGb = A(0, 4, BF16); QAG = QAG.rearrange("p (c t) -> p c t", c=4)
        KVG, KVGb = A(4, 4, BF16); KVG = KVG.rearrange("p (c t) -> p c t", c=4)
        RQA, RQAb = A(8, 2, F32); RQA2, RQA2b = A(10, 2, F32)
        RKV, RKVb = A(12, 2, F32); RKV2, RKV2b = A(14, 2, F32)
        KRR, KRRb = A(28, 2, F32); SSKR, SSKRb = A(30, 2, F32)
        QRR, QRRb = A(32, 2, F32)
        SQR, SQRb = A(42, 1, BF16)
        R1, R1b, R2_, R2b_ = self.XCH[0], [self.XCH_buf[0]], self.XCH[1], [self.XCH_buf[1]]
        pst, pstb = self.PST
        SL, SLb, RS, RSb = self.SL, [self.SL_buf], self.RS, [self.RS_buf]

        for grp, (dst, dstb, gt, Rt, Rtb, R2t, R2tb) in enumerate(((QAG, QAGb, self.gqa, RQA, RQAb, RQA2, RQA2b),
                                                                   (KVG, KVGb, self.gkva, RKV, RKVb, RKV2, RKV2b))):
            steps = []
            for half in range(2):
                def load(grp=grp, half=half):
                    return self.load_wa(w2d, grp * 512 + half * 256)

                def comp(hd, half=half, dst=dst, dstb=dstb, gt=gt):
                    w, wb = hd
                    for sub in range(2):
                        j = half * 2 + sub
                        ps, psb = self.ps_next()
                        self.mm_group(ps, psb, lambda k, w=w, sub=sub: w[:, k, sub * P:(sub + 1) * P], self.xn_rhs, DC, [wb] + self.XN_buf)
                        if DBG_STOP != 131:
                            self.stats_accum(ps[:], psb, j, 4)
                        if DBG_STOP != 132:
                            sc.add("dve", I("tensor_scalar", out=dst[:, j, :], in0=ps[:], scalar1=gt[:, j:j + 1], scalar2=None, op0=ALU.mult),
                                   reads=psb + cb, writes=[dstb[j]])
                steps.append((load, comp))
            self.pipeline(steps)
            if DBG_STOP in (13, 131, 132):
                return
            sc.add("act", I("activation", out=SL[:], in_=pst[:], func=AF.Sqrt, scale=1.0 / 512.0, bias=self.epsT[:]), reads=pstb + cb, writes=SLb)
            sc.add("dve", I("reciprocal", out=Rt, in_=SL[:]), reads=SLb, writes=Rtb)
            sc.add("dve", I("scalar_tensor_tensor", out=R2t, in0=SL[:], scalar=(EPS if grp == 0 else 1.0), in1=SL[:], op0=ALU.mult, op1=ALU.mult),
                   reads=SLb, writes=R2tb)

        if DBG_STOP == 1:
            return
        i = self.rr("wa", 4)
        wv = self.WA[i]
        src = w2d.rearrange("(c p) f -> p c f", p=P)
        for (d0, s0, n) in ((0, 1024, 64), (64, 1024, 64), (128, 1056, 32), (160, 1024, 32), (192, 1056, 32), (224, 1024, 32)):
            self.dma("pool", wv[:, :, d0:d0 + n], src[:, :, s0:s0 + n], [], [self.WA_buf[i]], self.WA_buf[i])
        pa, pab = self.ps_next()
        self.mm_group(pa, pab, lambda k: wv[:, k, 0:128], self.xn_rhs, DC, [self.WA_buf[i]] + self.XN_buf)
        pb, pbb = self.ps_next()
        self.mm_group(pb, pbb, lambda k: wv[:, k, 128:256], self.xn_rhs, DC, [self.WA_buf[i]] + self.XN_buf)
        self.dve_tt(R1[:], pa[:], CGk, ALU.mult, pab + CGkb, R1b)
        self.dve_tt(R2_[:], pb[:], SGk, ALU.mult, pbb + SGkb, R2b_)
        self.pool_tt(KRR, R1[:], R2_[:], ALU.add, R1b + R2b_, KRRb)
        self.stats_accum(pa[:], pab, 0, 1)
        sc.add("dve", I("tensor_scalar", out=SSKR, in0=pst[:], scalar1=0.5 / 192.0, scalar2=EPS, op0=ALU.mult, op1=ALU.add), reads=pstb, writes=SSKRb)
        self.dve_tt(SSKR, SSKR, RKV2, ALU.mult, SSKRb + RKV2b, SSKRb)
        self.dve_tt(KRR, KRR, RKV2, ALU.mult, KRRb + RKV2b, KRRb)
        self.dve_tt(KRR, KRR, RKV, ALU.mult, KRRb + RKVb, KRRb)

        if DBG_STOP == 2:
            return
        wqb = self.ab_w_q_b[0].rearrange("(c p) (h x) -> p c h x", p=P, x=QKH)
        iN = self.rr("wa", 4)
        WQN = self.waflat(iN).rearrange("p (c h x) -> p c h x", c=4, h=8)
        for c in range(4):
            self.dma("pool", WQN[:, c], wqb[:, c, :, 0:128], [], [self.WA_buf[iN]], self.WA_buf[iN])
        iR = self.rr("wa", 4)
        WQR = self.waflat(iR).rearrange("p (c v h x) -> p c v h x", c=4, v=2, h=8)
        for c in range(4):
            self.dma("pool", WQR[:, c, 0, :, :], wqb[:, c, :, 128:192], [], [self.WA_buf[iR]], self.WA_buf[iR])
            self.dma("pool", WQR[:, c, 1, :, 0:32], wqb[:, c, :, 160:192], [], [self.WA_buf[iR]], self.WA_buf[iR])
            self.dma("pool", WQR[:, c, 1, :, 32:64], wqb[:, c, :, 128:160], [], [self.WA_buf[iR]], self.WA_buf[iR])
        WQRf = self.waflat(iR).rearrange("p (c v x) -> p c v x", c=4, v=2)

        def qag_rhs(k, h):
            return QAG[:, k, h * HT:(h + 1) * HT]

        def kvg_rhs(k, h):
            return KVG[:, k, h * HT:(h + 1) * HT]
        for j in range(4):
            pa, pab = self.ps_next()
            self.mm_group(pa, pab, lambda k, j=j: WQRf[:, k, 0, j * P:(j + 1) * P], qag_rhs, 4, [self.WA_buf[iR]] + QAGb)
            pb, pbb = self.ps_next()
            self.mm_group(pb, pbb, lambda k, j=j: WQRf[:, k, 1, j * P:(j + 1) * P], qag_rhs, 4, [self.WA_buf[iR]] + QAGb)
            self.dve_tt(R1[:], pa[:], CGq, ALU.mult, pab + CGqb, R1b)
            self.dve_tt(R2_[:], pb[:], SGq, ALU.mult, pbb + SGqb, R2b_)
            self.pool_tt(QRR, R1[:], R2_[:], ALU.add, R1b + R2b_, QRRb)
            sc.add("act", I("activation", out=SQR, in_=pa[:], func=AF.Square), reads=pab, writes=SQRb)
            for hh in range(2):
                h = 2 * j + hh
                oi = self.rr("l0ro", 2)
                QRo, QRob = A(36 + oi, 1, BF16)
                pc, pcb = self.ps_next()
                self.mm_group(pc, pcb, lambda k, h=h: WQN[:, k, h, :], qag_rhs, 4, [self.WA_buf[iN]] + QAGb)
                sc.add("act", I("activation", out=self.SQ[:], in_=pc[:], func=AF.Square), reads=pcb, writes=[self.SQ_buf])

                def fn(e, hh=hh):
                    inst = None
                    for hf in range(2):
                        e.matmul(pst[:, hf * HT:(hf + 1) * HT], self.ones_bf[:], self.SQ[:, hf * HT:(hf + 1) * HT], start=True, stop=False)
                        inst = e.matmul(pst[:, hf * HT:(hf + 1) * HT], self.ehalf[:, hh, :], SQR[:, hf * HT:(hf + 1) * HT], start=False, stop=True)
                    return inst
                sc.add("pe", fn, reads=[self.SQ_buf] + SQRb + cb, writes=pstb)
                self.head_stats_chain(pst, pstb, RQA2, RQA2b)
                ai = self.rr("l0ao", 2)
                QAo, QAob = A(34 + ai, 1, BF16)
                sc.add("dve", I("scalar_tensor_tensor", out=QAo, in0=pc[:], scalar=self.gvec[:, 0:1], in1=RS[:], op0=ALU.mult, op1=ALU.mult),
                       reads=pcb + RSb + cb, writes=QAob)
                self.dma("sp", self.QAs[s, :, h, tsl], QAo, QAob, [self.QA_b[s][h][t]], QAob[0])
                r0, r1 = hh * 64, hh * 64 + 64
                o0, o1 = (1 - hh) * 64, (1 - hh) * 64 + 64
                sc.add("pool", I("memset", QRo[o0:o1, :], 0.0), writes=QRob)
                self.pool_tt(QRo[r0:r1, :], QRR[r0:r1, :], RS[r0:r1, :], ALU.mult, QRRb + RSb + QRob, QRob)
                self.dma("sp", self.QRs[s, :, h, tsl], QRo, QRob, [self.QR_b[s][h][t]], QRob[0])

        if DBG_STOP == 3:
            return
        wkv = self.ab_w_kv_b[0].rearrange("(c p) (h x) -> p c h x", p=P, x=256)
        iK = self.rr("wa", 4)
        WKN = self.waflat(iK).rearrange("p (c h x) -> p c h x", c=4, h=8)
        for c in range(4):
            self.dma("pool", WKN[:, c], wkv[:, c, :, 0:128], [], [self.WA_buf[iK]], self.WA_buf[iK])
        iV = self.rr("wa", 4)
        WV = self.waflat(iV).rearrange("p (c x) -> p c x", c=4)
        WV4 = self.waflat(iV).rearrange("p (c h x) -> p c h x", c=4, h=8)
        for c in range(4):
            self.dma("pool", WV4[:, c], wkv[:, c, :, 128:256], [], [self.WA_buf[iV]], self.WA_buf[iV])
        for j in range(4):
            oi = self.rr("l0ro", 2)
            KRo, KRob = A(36 + oi, 1, BF16)
            for hh in range(2):
                h = 2 * j + hh
                pc, pcb = self.ps_next()
                self.mm_group(pc, pcb, lambda k, h=h: WKN[:, k, h, :], kvg_rhs, 4, [self.WA_buf[iK]] + KVGb)
                self.stats_accum(pc[:], pcb, 0, 1)
                self.head_stats_chain(pst, pstb, SSKR, SSKRb)
                ai = self.rr("l0ao", 2)
                KAo, KAob = A(34 + ai, 1, BF16)
                sc.add("dve", I("scalar_tensor_tensor", out=KAo, in0=pc[:], scalar=self.gvec[:, 3:4], in1=RS[:], op0=ALU.mult, op1=ALU.mult),
                       reads=pcb + RSb + cb, writes=KAob)
                self.dma("sp", self.KAs[s, :, h, tsl], KAo, KAob, [self.KA_b[s][h][t]], KAob[0])
                r0, r1 = hh * 64, hh * 64 + 64
                self.pool_tt(KRo[r0:r1, :], KRR[r0:r1, :], RS[r0:r1, :], ALU.mult, KRRb + RSb, KRob)
            self.dma("sp", self.KRs[s, :, j, tsl], KRo, KRob, [self.KR_b[s][j][t]], KRob[0])

        if DBG_STOP == 4:
            return
        ps, psb = self.ps_next()

        def fnr(e):
            inst = None
            for b in range(8):
                inst = e.matmul(ps[:, 2 * b:2 * b + 2], RKV[0:1, b * P:(b + 1) * P], self.ones_f[0:1, 0:2], start=True, stop=True)
            return inst
        sc.add("pe", fnr, reads=RKVb + cb, writes=psb)
        sc.add("dve", I("tensor_copy", out=self.RKVT[:], in_=ps[:, 0:16]), reads=psb, writes=[self.RKVT_buf])
        for b in range(8):
            ps, psb = self.ps_next()

            def fnv(e, ps=ps, b=b):
                inst = None
                for k in range(4):
                    for i2 in range(2):
                        inst = e.matmul(ps[:, i2 * HT:(i2 + 1) * HT], KVG[:, k, b * P:(b + 1) * P], WV[:, k, i2 * HT:(i2 + 1) * HT],
                                        start=(k == 0), stop=(k == 3))
                return inst
            sc.add("pe", fnv, reads=KVGb + [self.WA_buf[iV]], writes=psb)
            vi = self.rr("l0vo", 2)
            Vo, Vob = A(38 + vi, 1, BF16)
            sc.add("act", I("activation", out=Vo, in_=ps[:], func=AF.Copy, scale=self.RKVT[:, 2 * b:2 * b + 1]),
                   reads=psb + [self.RKVT_buf], writes=Vob)
            self.dma("sp", self.Vs[s, t * T + b * P:t * T + (b + 1) * P, :], Vo, Vob, [self.V_b[s][0][t]], Vob[0])

        if DBG_STOP == 5:
            return
        steps = []
        for c2 in range(4):
            def load_b(c2=c2):
                return [self.load_wa(w2d, 1088 + c2 * 256)]

            def comp_b(hd, c2=c2):
                (w, wb), = hd
                for sub in range(2):
                    cc = c2 * 2 + sub
                    ps, psb = self.ps_next()
                    self.mm_group(ps, psb, lambda k, w=w, sub=sub: w[:, k, sub * P:(sub + 1) * P], self.xn_rhs, DC, [wb] + self.XN_buf)
                    X2, X2b = self.XCH[2], [self.XCH_buf[2]]
                    sc.add("act", I("activation", out=X2[:], in_=ps[:], func=AF.Copy), reads=psb, writes=X2b)
                    self.dma("sp", self.GBs[s, :, cc, tsl], X2[:], X2b, [self.GB_b[s][cc][t]], X2b[0])

            def load_cu(c2=c2):
                return [self.load_wa(w2d, 2112 + c2 * 256), self.load_wa(w2d, 3136 + c2 * 256)]

            def comp_cu(hd, c2=c2):
                (wc, wcb), (wu, wub) = hd
                for sub in range(2):
                    cc = c2 * 2 + sub
                    ps, psb = self.ps_next()
                    self.mm_group(ps, psb, lambda k, sub=sub: wc[:, k, sub * P:(sub + 1) * P], self.xn_rhs, DC, [wcb] + self.XN_buf)
                    sc.add("act", I("activation", out=SL[:], in_=ps[:], func=AF.Copy), reads=psb, writes=SLb)
                    ps, psb = self.ps_next()
                    self.mm_group(ps, psb, lambda k, sub=sub: wu[:, k, sub * P:(sub + 1) * P], self.xn_rhs, DC, [wub] + self.XN_buf)
                    X3, X3b = self.XCH[3], [self.XCH_buf[3]]
                    self.dve_tt(X3[:], ps[:], SL[:], ALU.mult, psb + SLb, X3b)
                    self.dma("sp", self.CUs[s, :, cc, tsl], X3[:], X3b, [self.CU_b[s][cc][t]], X3b[0])
            steps.append((load_b, comp_b))
            steps.append((load_cu, comp_cu))
        self.pipeline(steps)

    def ps_bank(self, b):
        return self.PS[b // 2][:, (b % 2) * HT:(b % 2 + 1) * HT], [self.PSB[b]]

    def l0_att(self, s):
        sc = self.sc
        A = self.arena
        S, NT = self.S, self.NT
        NQB, NKC = S // HT, S // P
        cb = [self.const_buf]
        scale = float(QKH) ** -0.5
        nch = S // 1024
        self.fence(40, 4)
        for h in range(8):
            slot = self.rr("attslot", 2)
            base = slot * 20
            KA, KAb = A(base, nch, BF16)
            KR, KRb = A(base + 4, nch, BF16)
            V, Vb = A(base + 8, nch, BF16)
            QA, QAb = A(base + 12, nch, BF16)
            QR, QRb = A(base + 16, nch, BF16)
            V3 = V.rearrange("p (c d) -> p c d", d=P)
            j = h // 2
            allt = range(NT)
            self.dma("sp", KA, self.KAs[s, :, h, :], [self.KA_b[s][h][t] for t in allt], KAb, KAb[0])
            self.dma("sp", KR, self.KRs[s, :, j, :], [self.KR_b[s][j][t] for t in allt], KRb, KRb[0])
            self.dma("sp", QA, self.QAs[s, :, h, :], [self.QA_b[s][h][t] for t in allt], QAb, QAb[0])
            self.dma("sp", QR, self.QRs[s, :, h, :], [self.QR_b[s][h][t] for t in allt], QRb, QRb[0])
            vsrc = self.Vs[s].rearrange("(c p) (h d) -> p c h d", p=P, d=P)
            for c4 in range(0, NKC, 8):
                self.dma("sp", V3[:, c4:c4 + 8, :], vsrc[:, c4:c4 + 8, h, :], [self.V_b[s][0][t] for t in allt], Vb, Vb[0])
            r0, r1 = (h % 2) * 64, (h % 2) * 64 + 64
            for qb in range(NQB):
                qsl = slice(qb * HT, (qb + 1) * HT)
                oi = self.rr("oacc", 2)
                O, Ob = self.ps_bank(2 * oi)
                DEN, DENb = self.ps_bank(2 * oi + 1)

                def emit_S(kc):
                    bank = 4 + self.rr("sbank", 4)
                    Sb_, Sbb = self.ps_bank(bank)
                    ksl = slice(kc * P, (kc + 1) * P)

                    def fn(e):
                        e.matmul(Sb_, KA[:, ksl], QA[:, qsl], start=True, stop=False)
                        return e.matmul(Sb_, KR[:, ksl], QR[:, qsl], start=False, stop=True)
                    sc.add("pe", fn, reads=KAb + KRb + QAb + QRb, writes=Sbb)
                    pi = self.rr("pt", 4)
                    PT, PTb = A(40, 2, BF16)
                    PTi = PT[:, pi * HT:(pi + 1) * HT]
                    PTib = [self.PT_bufs[pi]]
                    sc.add("act", I("activation", out=PTi, in_=Sb_, func=AF.Exp, scale=scale), reads=Sbb + PTb, writes=PTib)
                    return PTi, PTib

                def emit_PV(kc, pt):
                    PTi, PTib = pt

                    def fn(e):
                        e.matmul(O, V3[:, kc, :], PTi, start=(kc == 0), stop=(kc == NKC - 1))
                        return e.matmul(DEN, self.ones_bf[:], PTi, start=(kc == 0), stop=(kc == NKC - 1))
                    sc.add("pe", fn, reads=Vb + PTib + cb, writes=Ob + DENb)
                pts = {}
                for kc in range(min(2, NKC)):
                    pts[kc] = emit_S(kc)
                for kc in range(NKC):
                    if kc + 2 < NKC:
                        pts[kc + 2] = emit_S(kc + 2)
                    emit_PV(kc, pts.pop(kc))
                RD, RDb = A(42, 1, F32)
                sc.add("dve", I("reciprocal", out=RD, in_=DEN), reads=DENb, writes=RDb)
                ai = self.rr("atto", 2)
                AO, AOb_ = A(43, 1, BF16)
                AOi = AO[:, ai * HT:(ai + 1) * HT]
                AOib = [self.AO_bufs[ai]]
                self.dve_tt(AOi, O, RD, ALU.mult, Ob + RDb + AOb_, AOib)
                self.dma("sp", self.ATs[s, :, h, qsl], AOi, AOib, [self.AT_b[s][h][qb * HT // T]], AOib[0])

    def out_proj(self, s, t, w2d):
        steps = []
        for d2 in range(8):
            def load(d2=d2):
                return self.load_wa(w2d, d2 * 256)

            def comp(hd, d2=d2):
                w, wb = hd
                for sub in range(2):
                    dc = d2 * 2 + sub
                    ps, psb = self.ps_next()
                    self.mm_group(ps, psb, lambda k, w=w, sub=sub: w[:, k, sub * P:(sub + 1) * P], self.xn_rhs, DC, [wb] + self.XN_buf)
                    self.resid_step(s, t, dc, ps, psb, 1.0)
            steps.append((load, comp))
        self.pipeline(steps, ahead=2)

    def l0_post(self, s, t):
        sc = self.sc
        A = self.arena
        S, NT = self.S, self.NT
        cb = [self.const_buf]
        tsl = slice(t * T, (t + 1) * T)
        for h in range(8):
            self.dma("sp", self.XN[:, h, :], self.ATs[s, :, h, tsl], [self.AT_b[s][h][t]], [self.XN_buf[h]], self.XN_buf[h])
        SL, SLb = self.SL, [self.SL_buf]
        for cc in range(8):
            ci = self.rr("cu", 2)
            CUh, CUhb = A(ci * 3, 3, F32)
            GBt, GBtb = A(6 + 2 * ci, 2, F32)
            lo, hi = t * T - 1, t * T + T + 1
            a, b = 0, T + 2
            rd = [self.CU_b[s][cc][t]]
            if t == 0:
                sc.add("pool", I("memset", CUh[:, 0:1], 0.0), writes=CUhb)
                lo, a = 0, 1
            else:
                rd.append(self.CU_b[s][cc][t - 1])
            if t == NT - 1:
                sc.add("pool", I("memset", CUh[:, T + 1:T + 2], 0.0), writes=CUhb)
                hi, b = S, T + 1
            else:
                rd.append(self.CU_b[s][cc][t + 1])
            self.dma("sp", CUh[:, a:b], self.CUs[s, :, cc, lo:hi], rd, CUhb, CUhb[0])
            self.dma("sp", GBt, self.GBs[s, :, cc, tsl], [self.GB_b[s][cc][t]], GBtb, GBtb[0])
            w = self.convw
            sc.add("dve", I("tensor_scalar", out=SL[:], in0=CUh[:, 0:T], scalar1=w[:, cc, 0:1], scalar2=None, op0=ALU.mult),
                   reads=CUhb + cb, writes=SLb)
            for kk in (1, 2):
                sc.add("dve", I("scalar_tensor_tensor", out=SL[:], in0=CUh[:, kk:kk + T], scalar=w[:, cc, kk:kk + 1], in1=SL[:],
                                                                                    op0=ALU.mult, op1=ALU.add),
                       reads=CUhb + SLb + cb, writes=SLb)
            self.dve_tt(self.XN[:, 8 + cc, :], SL[:], GBt, ALU.mult, SLb + GBtb, [self.XN_buf[8 + cc]])
        self.out_proj(s, t, self.ab_w_out[0])

    def l1_pre(self, s, t):
        sc = self.sc
        A = self.arena
        cb = [self.const_buf]
        tsl = slice(t * T, (t + 1) * T)
        self.normalize(s, t, 5)
        (CGq, CGqb), (SGq, SGqb), (CGk, CGkb), (SGk, SGkb) = self.rope_tables(t, self.rope128, 0, (6, 7, 8, 9))
        w2d = self.c_w_in[0]
        src = w2d.rearrange("(c p) f -> p c f", p=P)
        pst, pstb = self.PST
        SL, SLb, RS, RSb = self.SL, [self.SL_buf], self.RS, [self.RS_buf]
        R1, R1b, R2_, R2b_, R3, R3b = self.XCH[0], [self.XCH_buf[0]], self.XCH[1], [self.XCH_buf[1]], self.XCH[2], [self.XCH_buf[2]]
        steps = []
        for hp in range(10):
            def load(hp=hp):
                c0 = hp * 256
                wn = self.load_wa(w2d, c0)
                i = self.rr("wa", 4)
                wv = self.WA[i]
                for (d0, s0) in ((0, 64), (64, 0), (128, 192), (192, 128)):
                    self.dma("pool", wv[:, :, d0:d0 + 64], src[:, :, c0 + s0:c0 + s0 + 64], [], [self.WA_buf[i]], self.WA_buf[i])
                return wn, (wv, self.WA_buf[i])

            def comp(hd, hp=hp):
                (wn, wnb), (ws, wsb) = hd
                isq = hp < 8
                CG, CGb, SG, SGb = (CGq, CGqb, SGq, SGqb) if isq else (CGk, CGkb, SGk, SGkb)
                for sub in range(2):
                    hidx = hp * 2 + sub if isq else (hp - 8) * 2 + sub
                    pa, pab = self.ps_next()
                    self.mm_group(pa, pab, lambda k, sub=sub: wn[:, k, sub * P:(sub + 1) * P], self.xn_rhs, DC, [wnb] + self.XN_buf)
                    pb, pbb = self.ps_next()
                    self.mm_group(pb, pbb, lambda k, sub=sub: ws[:, k, sub * P:(sub + 1) * P], self.xn_rhs, DC, [wsb] + self.XN_buf)
                    self.stats_accum(pa[:], pab, 0, 1)
                    sc.add("act", I("activation", out=SL[:], in_=pst[:], func=AF.Sqrt, scale=1.0 / 128.0, bias=self.epsT[:]),
                           reads=pstb + cb, writes=SLb)
                    sc.add("dve", I("reciprocal", out=RS[:], in_=SL[:]), reads=SLb, writes=RSb)
                    self.dve_tt(R1[:], pa[:], CG, ALU.mult, pab + CGb, R1b)
                    self.dve_tt(R2_[:], pb[:], SG, ALU.mult, pbb + SGb, R2b_)
                    self.pool_tt(R3[:], R1[:], R2_[:], ALU.add, R1b + R2b_, R3b)
                    oi = self.rr("l1qo", 2)
                    Qo, Qob = A(12 + oi, 1, BF16)
                    self.pool_tt(Qo, R3[:], RS[:], ALU.mult, R3b + RSb, Qob)
                    if isq:
                        self.dma("sp", self.Q1s[s, :, hidx, tsl], Qo, Qob, [self.Q1_b[s][hidx][t]], Qob[0])
                    else:
                        self.dma("sp", self.K1s[s, :, hidx, tsl], Qo, Qob, [self.K1_b[s][hidx][t]], Qob[0])
            steps.append((load, comp))
        self.pipeline(steps)
        self.fence(14, 1)
        wv0 = self.load_wa(w2d, 2560)
        wv1 = self.load_wa(w2d, 2816)
        for b in range(8):
            ps, psb = self.ps_next()

            def fnv(e, ps=ps, b=b):
                inst = None
                for k in range(DC):
                    for i2, (w, wb) in enumerate((wv0, wv1)):
                        inst = e.matmul(ps[:, i2 * HT:i2 * HT + 256], self.XN[:, k, b * P:(b + 1) * P], w[:, k, :], start=(k == 0), stop=(k == DC - 1))
                return inst
            sc.add("pe", fnv, reads=self.XN_buf + [wv0[1], wv1[1]], writes=psb)
            vi = self.rr("l1vo", 2)
            Vo, Vob = A(14, 1, BF16)
            Voi = Vo[:, vi * HT:(vi + 1) * HT]
            Voib = [self.VO_bufs[vi]]
            sc.add("act", I("activation", out=Voi.rearrange("p (a b) -> p a b", a=2), in_=ps[:].rearrange("p (a b) -> p a b", a=2)[:, :, 0:256], func=AF.Copy),
                   reads=psb + Vob, writes=Voib)
            self.dma("sp", self.V1s[s, t * T + b * P:t * T + (b + 1) * P, :], Voi, Voib, [self.V1_b[s][0][t]], Voib[0])

    def l1_att(self, s):
        sc = self.sc
        A = self.arena
        S, NT = self.S, self.NT
        NB = S // P
        cb = [self.const_buf]
        scale = 128.0 ** -0.5
        nch = S // 1024
        allt = range(NT)
        self.fence(24, 6)
        ES, ESb = A(30, 4, F32)
        ES3 = ES.rearrange("p (h q) -> p h q", h=16)
        for h in range(16):
            sc.add("act", I("activation", out=ES3[:, h, :], in_=self.zeros_f[:], func=AF.Identity, bias=self.esink[:, h:h + 1]),
                   reads=cb, writes=ESb)
        for g in range(4):
            K1t, K1b = A(0, nch, BF16)
            V1t, V1b = A(4, nch, BF16)
            Qt, Qb = A(8, 4 * nch, BF16)
            V3 = V1t.rearrange("p (c d) -> p c d", d=P)
            Q3 = Qt.rearrange("p (h q) -> p h q", h=4)
            self.dma("sp", K1t, self.K1s[s, :, g, :], [self.K1_b[s][g][t] for t in allt], K1b, K1b[0])
            vsrc = self.V1s[s].rearrange("(c p) (g d) -> p c g d", p=P, d=P)
            for c4 in range(0, NB, 8):
                self.dma("sp", V3[:, c4:c4 + 8, :], vsrc[:, c4:c4 + 8, g, :], [self.V1_b[s][0][t] for t in allt], V1b, V1b[0])
            for hh in range(4):
                self.dma("sp", Q3[:, hh, :], self.Q1s[s, :, 4 * g + hh, :], [self.Q1_b[s][4 * g + hh][t] for t in allt], Qb, Qb[0])
            for qb in range(NB):
                qsl = slice(qb * P, (qb + 1) * P)
                kbs = [kb for kb in (qb - 1, qb, qb + 1) if 0 <= kb < NB]
                oi = self.rr("oacc", 2)
                O, Ob = self.ps_bank(2 * oi)
                DEN, DENb = self.ps_bank(2 * oi + 1)
                pts = []
                for kb in kbs:
                    bank = 4 + self.rr("sbank", 4)
                    Sb_, Sbb = self.ps_bank(bank)
                    sc.add("pe", I("matmul", Sb_, K1t[:, kb * P:(kb + 1) * P], Q3[:, :, qsl], start=True, stop=True),
                           reads=K1b + Qb, writes=Sbb)
                    pi = self.rr("pt1", 6)
                    PT, PTb = A(24, 3, BF16)
                    PTi = PT[:, pi * HT:(pi + 1) * HT]
                    PTib = [self.PT1_bufs[pi]]
                    sc.add("act", I("activation", out=PTi, in_=Sb_, func=AF.Exp, scale=scale), reads=Sbb + PTb, writes=PTib)
                    if kb != qb:
                        w = 0 if kb < qb else 1
                        M = self.MASK[:, w, :, :].rearrange("p r q -> p (r q)")
                        self.pool_tt(PTi, PTi, M, ALU.mult, PTib + cb + PTb, PTib)
                    pts.append((kb, PTi, PTib))
                for i, (kb, PTi, PTib) in enumerate(pts):
                    def fn(e, i=i, kb=kb, PTi=PTi):
                        e.matmul(O, V3[:, kb, :], PTi, start=(i == 0), stop=(i == len(pts) - 1))
                        return e.matmul(DEN, self.ones_bf[:], PTi, start=(i == 0), stop=(i == len(pts) - 1))
                    sc.add("pe", fn, reads=V1b + PTib + cb, writes=Ob + DENb)
                TD, TDb = A(27, 1, F32)
                RD, RDb = A(28, 1, F32)
                self.dve_tt(TD, DEN, ES3[:, 4 * g:4 * g + 4, :].rearrange("p h q -> p (h q)"), ALU.add, DENb + ESb, TDb)
                sc.add("dve", I("reciprocal", out=RD, in_=TD), reads=TDb, writes=RDb)
                ai = self.rr("att1o", 2)
                AO, AOb_ = A(29, 1, BF16)
                AOi = AO[:, ai * HT:(ai + 1) * HT]
                AOib = [self.AO1_bufs[ai]]
                self.dve_tt(AOi, O, RD, ALU.mult, Ob + RDb + AOb_, AOib)
                self.dma("sp", self.A1s[s, :, 4 * g:4 * g + 4, qsl], AOi.rearrange("p (h q) -> p h q", h=4), AOib,
                         [self.A1_b[s][g][qb * P // T]], AOib[0])

    def l1_post(self, s, t):
        tsl = slice(t * T, (t + 1) * T)
        for h in range(16):
            self.dma("sp", self.XN[:, h, :], self.A1s[s, :, h, tsl], [self.A1_b[s][h // 4][t]], [self.XN_buf[h]], self.XN_buf[h])
        self.out_proj(s, t, self.c_w_out[0])

    def build(self):
        self.declare()
        self.load_consts()
        self.load_consts2()
        st = self.stages
        NTr = range(self.NT)
        for s in range(self.nseq):
            if st == "ffn1":
                for t in NTr:
                    self.tr_in(s, t)
                    self.ffn(s, t, 0, 0, do_stats=False)
                    self.tr_out(s, t)
                continue
            if st == "none":
                for t in NTr:
                    self.tr_in(s, t)
                    self.tr_out(s, t)
                continue
            lvl = 9 if st == "all" else float(st)
            for t in NTr:
                self.tr_in(s, t)
                self.ffn(s, t, 0, 0)
                self.l0_pre(s, t)
            if lvl >= 0.5:
                self.l0_att(s)
            for t in NTr:
                if lvl >= 1:
                    self.l0_post(s, t)
                if lvl >= 2:
                    self.ffn(s, t, 0, 1)
                    self.ffn(s, t, 1, 0)
                    self.l1_pre(s, t)
            if lvl >= 2:
                self.l1_att(s)
            for t in NTr:
                if lvl >= 2:
                    self.l1_post(s, t)
                    self.ffn(s, t, 1, 1, do_stats=False)
                self.tr_out(s, t)
        self.sc.emit(self.nc, self.stack)
        self.stack.close()
        return self.nc


def host_consts(S):
    pos = np.arange(S, dtype=np.float32)
    out = {}
    for dim, name, rep in ((64, "rope64", 2), (128, "rope128", 1)):
        half = dim // 2
        inv = (1.0 / (10000.0 ** (np.arange(0, dim, 2, dtype=np.float32) / np.float32(dim)))).astype(np.float32)
        ang = (pos[:, None] * inv[None, :]).astype(np.float32)
        cos = np.cos(ang).astype(np.float32).T
        sin = np.sin(ang).astype(np.float32).T
        c = np.concatenate([cos, cos] * rep, axis=0)
        sg = np.concatenate([-sin, sin] * rep, axis=0)
        out[name] = np.ascontiguousarray(np.stack([c, sg]).astype(np.float32))
    j = np.arange(128)[:, None]
    i = np.arange(128)[None, :]
    out["masks"] = np.stack([(j >= i), (j <= i)]).astype(np.float32)
    out["ident_in"] = np.eye(128, dtype=np.float32)
    return out


WEIGHT_NAMES = ["ffn_norm", "ffn_w_gate", "ffn_w_up", "ffn_w_down", "mix_norm", "ab_w_in", "ab_q_a_norm", "ab_w_q_b",
                "ab_kv_a_norm", "ab_w_kv_b", "ab_q_norm", "ab_k_norm", "ab_conv_w", "ab_w_out", "c_w_in", "c_q_norm",
                "c_k_norm", "c_sink", "c_w_out"]


def kernel(x_prompt, x_sample, **weights):
    seqs = [np.asarray(x_prompt[i]) for i in range(x_prompt.shape[0])] + [np.asarray(x_sample[i]) for i in range(x_sample.shape[0])]
    nseq_tot = len(seqs)
    ncores = 8
    S = seqs[0].shape[0]
    assign = []
    for c in range(ncores):
        a = [c, c + 8 if c + 8 < nseq_tot else c]
        assign.append(a)
    b = Builder(2, S, stages="all")
    nc = b.build()
    consts = host_consts(S)
    wts = {k: np.ascontiguousarray(np.asarray(weights[k], dtype=np.float32)) for k in WEIGHT_NAMES}
    in_maps = []
    for c in range(ncores):
        m = {"x": np.ascontiguousarray(np.stack([seqs[i] for i in assign[c]]).astype(np.float32))}
        m.update(wts)
        m.update(consts)
        in_maps.append(m)
    res = run_bass_kernel_spmd(nc, in_maps, core_ids=list(range(ncores)))
    outs = [None] * nseq_tot
    for c in range(ncores):
        y = res.results[c]["y"]
        for k, i in enumerate(assign[c]):
            if outs[i] is None:
                outs[i] = y[k]
    npr = x_prompt.shape[0]
    return (np.stack(outs[:npr]).astype(np.float32), np.stack(outs[npr:]).astype(np.float32))
```
